# Optimizing a Trainium2 kernel written in Bass

```python
import math
import jax
import jax.numpy as jnp
from jax import lax
import numpy as np

D_MODEL = 1024
BATCH = 32
SEQ = 2048
DEPTH = 4

GRID_W = 64
CTX_LEN = 256
N_MIXERS = 2
N_CONV_LAYERS = (DEPTH + N_MIXERS - 1) // N_MIXERS
N_ATTN_LAYERS = DEPTH // N_MIXERS
CONV_WIDTH = 31
DA_HEAD_DIM = 64
DA_V_DIM = 2 * DA_HEAD_DIM
DA_HEADS = D_MODEL // DA_V_DIM
DA_QK_WIDTH = DA_HEADS * 2 * DA_HEAD_DIM
DA_SCALE = DA_HEAD_DIM ** -0.5
ROPE_THETA = 10000.0
Q_BLOCK = 128
D_FF = ((8 * D_MODEL // 3 + 127) // 128) * 128
FFN_CONV_WIDTH = 3
EPS = 1e-6

kernel_name = "hybrid_conformer_diffattn_prefix_dit"


def rms_norm(x, g):
    xf = x.astype(jnp.float32)
    y = xf * lax.rsqrt(jnp.mean(jnp.square(xf), axis=-1, keepdims=True) + EPS)
    return (y * g.astype(jnp.float32)).astype(x.dtype)


def layer_norm(x, g, b):
    xf = x.astype(jnp.float32)
    mu = jnp.mean(xf, axis=-1, keepdims=True)
    var = jnp.mean(jnp.square(xf - mu), axis=-1, keepdims=True)
    y = (xf - mu) * lax.rsqrt(var + EPS)
    return (y * g.astype(jnp.float32) + b.astype(jnp.float32)).astype(x.dtype)


def modulate(h, shift, scale):
    return h * (1 + scale) + shift


def dwconv_same(u, w, b):
    k = w.shape[0]
    pad = (k - 1) // 2
    y = lax.conv_general_dilated(u, w[:, None, :].astype(u.dtype), window_strides=(1,),
                                 padding=((pad, pad),), dimension_numbers=("NWC", "WIO", "NWC"),
                                 feature_group_count=u.shape[-1])
    return y + b


def token_dwconv(u, w, b, on_grid):
    if on_grid:
        bsz, n, ch = u.shape
        rows = n // GRID_W
        return dwconv_same(u.reshape(bsz * rows, GRID_W, ch), w, b).reshape(bsz, n, ch)
    return dwconv_same(u, w, b)


def conformer_conv(h, pw1_w, pw1_b, dw_w, dw_b, ln_g, ln_b, pw2_w, pw2_b, on_grid):
    u = h @ pw1_w + pw1_b
    a, g = jnp.split(u, 2, axis=-1)
    u = a * jax.nn.sigmoid(g)
    u = token_dwconv(u, dw_w, dw_b, on_grid)
    u = jax.nn.silu(layer_norm(u, ln_g, ln_b))
    return u @ pw2_w + pw2_b


def conv_ffn(h, w_up, dw_w, dw_b, w_down, on_grid):
    u = h @ w_up
    u = token_dwconv(u, dw_w, dw_b, on_grid)
    a, v = jnp.split(u, 2, axis=-1)
    return (jax.nn.silu(a) * v) @ w_down


def axial_rope_tables(n_tok):
    t = jnp.arange(n_tok)
    row = (t // GRID_W).astype(jnp.float32)
    col = (t % GRID_W).astype(jnp.float32)
    half = DA_HEAD_DIM // 2
    inv = ROPE_THETA ** (-jnp.arange(0, half, 2, dtype=jnp.float32) / half)
    ang_r = row[:, None] * inv
    ang_c = col[:, None] * inv
    ang = jnp.concatenate([ang_r, ang_r, ang_c, ang_c], axis=-1)
    return jnp.cos(ang)[:, None, None, :], jnp.sin(ang)[:, None, None, :]


def apply_axial_rope(x, cos, sin):
    xs = x.reshape(x.shape[:-1] + (2, 2, DA_HEAD_DIM // 4))
    rot = jnp.concatenate([-xs[..., 1:2, :], xs[..., 0:1, :]], axis=-2).reshape(x.shape)
    return x * cos.astype(x.dtype) + rot * sin.astype(x.dtype)


def diff_attend(q, k, v, lam):
    s = jnp.einsum("bqhcd,bkhcd->bhcqk", q, k, preferred_element_type=jnp.float32) * DA_SCALE
    p = jax.nn.softmax(s, axis=-1)
    a = p[:, :, 0] - lam * p[:, :, 1]
    return jnp.einsum("bhqk,bkhe->bqhe", a.astype(v.dtype), v)


def diff_attention(h_l, h_c, wqkv, wo, qn_g, kn_g, lq1, lk1, lq2, lk2, subln_g, lam_init, cos, sin, need_ctx):
    bsz, n_lat, _ = h_l.shape
    n_ctx = h_c.shape[1]
    f32 = jnp.float32
    lam = (jnp.exp(jnp.sum(lq1.astype(f32) * lk1.astype(f32)))
           - jnp.exp(jnp.sum(lq2.astype(f32) * lk2.astype(f32))) + lam_init)

    def qk_heads(p, L, g):
        return rms_norm(p.reshape(bsz, L, DA_HEADS, 2, DA_HEAD_DIM), g)

    p_l = h_l @ wqkv
    q_l = apply_axial_rope(qk_heads(p_l[..., :DA_QK_WIDTH], n_lat, qn_g), cos, sin)
    k_l = apply_axial_rope(qk_heads(p_l[..., DA_QK_WIDTH:2 * DA_QK_WIDTH], n_lat, kn_g), cos, sin)
    v_l = p_l[..., 2 * DA_QK_WIDTH:].reshape(bsz, n_lat, DA_HEADS, DA_V_DIM)

    p_c = h_c @ (wqkv if need_ctx else wqkv[:, DA_QK_WIDTH:])
    if need_ctx:
        q_c = qk_heads(p_c[..., :DA_QK_WIDTH], n_ctx, qn_g)
        p_c = p_c[..., DA_QK_WIDTH:]
    k_c = qk_heads(p_c[..., :DA_QK_WIDTH], n_ctx, kn_g)
    v_c = p_c[..., DA_QK_WIDTH:].reshape(bsz, n_ctx, DA_HEADS, DA_V_DIM)

    k_all = jnp.concatenate([k_c, k_l], axis=1)
    v_all = jnp.concatenate([v_c, v_l], axis=1)
    nb = n_lat // Q_BLOCK
    qb = q_l.reshape(bsz, nb, Q_BLOCK, DA_HEADS, 2, DA_HEAD_DIM).swapaxes(0, 1)
    o_l = lax.map(lambda q: diff_attend(q, k_all, v_all, lam), qb)
    o_l = o_l.swapaxes(0, 1).reshape(bsz, n_lat, DA_HEADS, DA_V_DIM)

    def out(o, L):
        o = rms_norm(o, subln_g) * (1 - lam_init)
        return o.reshape(bsz, L, DA_HEADS * DA_V_DIM) @ wo

    y_l = out(o_l, n_lat)
    y_c = out(diff_attend(q_c, k_c, v_c, lam), n_ctx) if need_ctx else None
    return y_l, y_c


def setup_inputs(seed: int = 0) -> dict:
    key = jax.random.key(seed)
    ks = iter(jax.random.split(key, 32))
    f32 = jnp.float32
    D = D_MODEL
    NC, NA = N_CONV_LAYERS, N_ATTN_LAYERS

    def nrm(shape, scale):
        return jax.random.normal(next(ks), shape, f32) * scale

    return {
        "x": nrm((BATCH, SEQ, D), 1.0),
        "c": nrm((BATCH, D), 1.0),
        "ctx": nrm((BATCH, CTX_LEN, D), 1.0),
        "c_ctx": nrm((D,), 1.0),
        "ada_w": nrm((DEPTH, D, 6 * D), 0.5 * D ** -0.5),
        "ada_b": nrm((DEPTH, 6 * D), 0.01),
        "mix_norm_g": 1.0 + nrm((DEPTH, D), 0.02),
        "ffn_norm_g": 1.0 + nrm((DEPTH, D), 0.02),
        "cv_pw1_w": nrm((NC, D, 2 * D), D ** -0.5),
        "cv_pw1_b": nrm((NC, 2 * D), 0.01),
        "cv_dw_w": nrm((NC, CONV_WIDTH, D), CONV_WIDTH ** -0.5),
        "cv_dw_b": nrm((NC, D), 0.01),
        "cv_ln_g": 1.0 + nrm((NC, D), 0.02),
        "cv_ln_b": nrm((NC, D), 0.01),
        "cv_pw2_w": nrm((NC, D, D), D ** -0.5),
        "cv_pw2_b": nrm((NC, D), 0.01),
        "da_wqkv": nrm((NA, D, 2 * DA_QK_WIDTH + DA_HEADS * DA_V_DIM), D ** -0.5),
        "da_wo": nrm((NA, DA_HEADS * DA_V_DIM, D), (DA_HEADS * DA_V_DIM) ** -0.5),
        "da_qn_g": 1.0 + nrm((NA, DA_HEAD_DIM), 0.02),
        "da_kn_g": 1.0 + nrm((NA, DA_HEAD_DIM), 0.02),
        "da_lq1": nrm((NA, DA_HEAD_DIM), 0.1),
        "da_lk1": nrm((NA, DA_HEAD_DIM), 0.1),
        "da_lq2": nrm((NA, DA_HEAD_DIM), 0.1),
        "da_lk2": nrm((NA, DA_HEAD_DIM), 0.1),
        "da_subln_g": 1.0 + nrm((NA, DA_V_DIM), 0.02),
        "ffn_w_up": nrm((DEPTH, D, 2 * D_FF), D ** -0.5),
        "ffn_dw_w": nrm((DEPTH, FFN_CONV_WIDTH, 2 * D_FF), FFN_CONV_WIDTH ** -0.5),
        "ffn_dw_b": nrm((DEPTH, 2 * D_FF), 0.01),
        "ffn_w_down": nrm((DEPTH, D_FF, D), D_FF ** -0.5),
    }


def reference(x, c, ctx, c_ctx, ada_w, ada_b, mix_norm_g, ffn_norm_g,
              cv_pw1_w, cv_pw1_b, cv_dw_w, cv_dw_b, cv_ln_g, cv_ln_b, cv_pw2_w, cv_pw2_b,
              da_wqkv, da_wo, da_qn_g, da_kn_g, da_lq1, da_lk1, da_lq2, da_lk2, da_subln_g,
              ffn_w_up, ffn_dw_w, ffn_dw_b, ffn_w_down):
    n_lat = x.shape[1]
    cos, sin = axial_rope_tables(n_lat)
    sc = jax.nn.silu(c)
    sc_ctx = jax.nn.silu(c_ctx)
    for i in range(DEPTH):
        last = i == DEPTH - 1
        j = i // N_MIXERS
        is_conv = (i % N_MIXERS) == 0
        mod_l = (sc @ ada_w[i] + ada_b[i])[:, None, :]
        mod_c = sc_ctx @ ada_w[i] + ada_b[i]
        sh_ml, sc_ml, g_ml, sh_fl, sc_fl, g_fl = jnp.split(mod_l, 6, axis=-1)
        sh_mc, sc_mc, g_mc, sh_fc, sc_fc, g_fc = jnp.split(mod_c, 6, axis=-1)

        h_l = modulate(rms_norm(x, mix_norm_g[i]), sh_ml, sc_ml)
        if is_conv:
            cp = (cv_pw1_w[j], cv_pw1_b[j], cv_dw_w[j], cv_dw_b[j], cv_ln_g[j], cv_ln_b[j], cv_pw2_w[j], cv_pw2_b[j])
            y_l = conformer_conv(h_l, *cp, on_grid=True)
            if not last:
                h_c = modulate(rms_norm(ctx, mix_norm_g[i]), sh_mc, sc_mc)
                y_c = conformer_conv(h_c, *cp, on_grid=False)
        else:
            h_c = modulate(rms_norm(ctx, mix_norm_g[i]), sh_mc, sc_mc)
            lam_init = 0.8 - 0.6 * math.exp(-0.3 * i)
            y_l, y_c = diff_attention(h_l, h_c, da_wqkv[j], da_wo[j], da_qn_g[j], da_kn_g[j],
                                      da_lq1[j], da_lk1[j], da_lq2[j], da_lk2[j], da_subln_g[j],
                                      lam_init, cos, sin, need_ctx=not last)

        x = x + g_ml * y_l
        f_l = modulate(rms_norm(x, ffn_norm_g[i]), sh_fl, sc_fl)
        x = x + g_fl * conv_ffn(f_l, ffn_w_up[i], ffn_dw_w[i], ffn_dw_b[i], ffn_w_down[i], on_grid=True)

        if not last:
            ctx = ctx + g_mc * y_c
            f_c = modulate(rms_norm(ctx, ffn_norm_g[i]), sh_fc, sc_fc)
            ctx = ctx + g_fc * conv_ffn(f_c, ffn_w_up[i], ffn_dw_w[i], ffn_dw_b[i], ffn_w_down[i], on_grid=False)
    return x
```

```python
import math
from contextlib import ExitStack

import numpy as np
import concourse.bass as bass
import concourse.mybir as mybir
from concourse.bass_utils import run_bass_kernel_spmd

F32 = mybir.dt.float32
BF16 = mybir.dt.bfloat16
AF = mybir.ActivationFunctionType
ALU = mybir.AluOpType

D = 1024
NCH = 8
SEQ = 2048
CTX = 256
NTOK = SEQ + CTX
DEPTH = 4
DFF = 2816
NFC = 22
EPS = 1e-6
DA_SCALE = 0.125
NSTREAM = 5


class Res:
    __slots__ = ("w", "rd")

    def __init__(self):
        self.w = None
        self.rd = {}


ENGS = ("pe", "act", "dve", "pool", "sp")


class Planner:
    ROLL = 30000
    NDSEM = 8

    def __init__(self, nc, es, dry=False):
        self.nc, self.es, self.dry = nc, es, dry
        self.ops = {e: [] for e in ENGS}
        self.sem = {e: None for e in ENGS}
        self.cnt = {e: 0 for e in ENGS}
        self.seen = {e: {} for e in ENGS}
        self.last = {e: None for e in ENGS}
        self.dsem = {"sp": [], "pool": []}
        self.dlast = {"sp": [None] * self.NDSEM, "pool": [None] * self.NDSEM}
        self.di = {"sp": 0, "pool": 0}
        self.nsem = 0
        self.nops = 0

    def _newsem(self):
        self.nsem += 1
        if self.dry:
            return ("sem", self.nsem)
        return self.es.enter_context(self.nc.semaphore("s%d" % self.nsem))

    def _waits(self, eng, deps):
        out = []
        seen = self.seen[eng]
        for ev in deps:
            if ev is None:
                continue
            sem, val, _ = ev
            k = id(sem)
            if seen.get(k, 0) >= val:
                continue
            seen[k] = val
            out.append((sem, val))
        return out

    def _deps(self, eng, r, w):
        deps = []
        for res in r:
            if res.w is not None:
                if not (res.w[2] == eng and eng == "pe"):
                    deps.append(res.w)
        for res in w:
            if res.w is not None and res.w[2] != eng:
                deps.append(res.w)
            for ev in res.rd.values():
                if ev[2] != eng:
                    deps.append(ev)
        return deps

    def _commit(self, ev, r, w):
        for res in r:
            res.rd[ev[2]] = ev
        for res in w:
            res.w = ev
            res.rd = {}

    def op(self, eng, fn, r=(), w=()):
        deps = self._deps(eng, r, w)
        if self.sem[eng] is None or self.cnt[eng] >= self.ROLL:
            self.sem[eng] = self._newsem()
            self.cnt[eng] = 0
        self.cnt[eng] += 1
        ev = (self.sem[eng], self.cnt[eng], eng)
        self.ops[eng].append((fn, self._waits(eng, deps), (self.sem[eng], 1)))
        self.last[eng] = ev
        self._commit(ev, r, w)
        self.nops += 1
        return ev

    def dma(self, q, fn, r=(), w=()):
        deps = self._deps(q, r, w)
        if not self.dsem[q]:
            self.dsem[q] = [self._newsem() for _ in range(self.NDSEM)]
        i = self.di[q]
        self.di[q] += 1
        slot = i % self.NDSEM
        deps.append(self.dlast[q][slot])
        ev = (self.dsem[q][slot], 16 * (i // self.NDSEM + 1), "dma_" + q + str(slot))
        self.dlast[q][slot] = ev
        self.ops[q].append((fn, self._waits(q, deps), (ev[0], 16)))
        self._commit(ev, r, w)
        self.nops += 1
        return ev

    def all_events(self):
        evs = [self.last[e] for e in ENGS if self.last[e] is not None]
        for q in ("sp", "pool"):
            evs += [e for e in self.dlast[q] if e is not None]
        return evs

    def barrier(self, engines=ENGS):
        evs = self.all_events()
        for e in engines:
            ws = self._waits(e, [ev for ev in evs if ev[2] != e])
            if ws:
                self.ops[e].append((None, ws, None))

    def emit(self):
        nc = self.nc
        ops = self.ops

        def run(e, name):
            for fn, waits, inc in ops[name]:
                for sem, val in waits:
                    e.wait_ge(sem, val)
                if fn is not None:
                    ins = fn(e)
                    ins.then_inc(inc[0], inc[1])

        with nc.Block() as block:
            @block.tensor
            def _(e):
                run(e, "pe")

            @block.scalar
            def _(e):
                run(e, "act")

            @block.vector
            def _(e):
                run(e, "dve")

            @block.gpsimd
            def _(e):
                run(e, "pool")

            @block.sync
            def _(e):
                run(e, "sp")


def _col(v, nchunk):
    return np.ascontiguousarray(v.reshape(nchunk, 128).T)


class ColPack:
    def __init__(self):
        self.blocks, self.off, self.n = [], {}, 0

    def add(self, name, arr):
        arr = np.asarray(arr, np.float32)
        assert arr.shape[0] == 128
        self.off[name] = (self.n, arr.shape[1])
        self.blocks.append(arr)
        self.n += arr.shape[1]


def col_layout():
    off, n = {}, 0

    def add(name, wd):
        nonlocal n
        off[name] = (n, wd)
        n += wd

    add("cT", NCH * NSTREAM)
    add("ada_b", DEPTH * 48)
    for l in range(DEPTH):
        add("mixg%d" % l, 8)
        add("ffng%d" % l, 8)
        add("fdw%d" % l, 44 * 3)
        add("fdb%d" % l, 44)
    for j in range(2):
        add("pw1b%d" % j, 16)
        add("dww%d" % j, 8 * 31)
        add("dwb%d" % j, 8)
        add("lng%d" % j, 8)
        add("lnb%d" % j, 8)
        add("pw2b%d" % j, 8)
        add("qng%d" % j, 1)
        add("kng%d" % j, 1)
        add("subg%d" % j, 1)
        for nm in ("lq1", "lk1", "lq2", "lk2"):
            add("%s%d" % (nm, j), 64)
    return off, n


def pack_cols(inp, c_loc):
    cp = ColPack()
    call = np.concatenate([c_loc, inp["c_ctx"][None, :]], axis=0)
    cp.add("cT", call.reshape(NSTREAM, NCH, 128).transpose(2, 1, 0).reshape(128, NCH * NSTREAM))
    cp.add("ada_b", inp["ada_b"].reshape(DEPTH, 48, 128).transpose(2, 0, 1).reshape(128, DEPTH * 48))
    for l in range(DEPTH):
        cp.add("mixg%d" % l, _col(inp["mix_norm_g"][l], 8))
        cp.add("ffng%d" % l, _col(inp["ffn_norm_g"][l], 8))
        cp.add("fdw%d" % l, inp["ffn_dw_w"][l].reshape(3, 44, 128).transpose(2, 1, 0).reshape(128, 132))
        cp.add("fdb%d" % l, _col(inp["ffn_dw_b"][l], 44))
    for j in range(2):
        cp.add("pw1b%d" % j, _col(inp["cv_pw1_b"][j], 16))
        cp.add("dww%d" % j, inp["cv_dw_w"][j].reshape(31, 8, 128).transpose(2, 1, 0).reshape(128, 248))
        cp.add("dwb%d" % j, _col(inp["cv_dw_b"][j], 8))
        cp.add("lng%d" % j, _col(inp["cv_ln_g"][j], 8))
        cp.add("lnb%d" % j, _col(inp["cv_ln_b"][j], 8))
        cp.add("pw2b%d" % j, _col(inp["cv_pw2_b"][j], 8))
        cp.add("qng%d" % j, np.tile(inp["da_qn_g"][j], 2)[:, None])
        cp.add("kng%d" % j, np.tile(inp["da_kn_g"][j], 2)[:, None])
        cp.add("subg%d" % j, inp["da_subln_g"][j][:, None])
        for nm in ("lq1", "lk1", "lq2", "lk2"):
            cp.add("%s%d" % (nm, j), np.broadcast_to(inp["da_" + nm][j][None, :], (128, 64)))
    off, n = col_layout()
    assert off == cp.off and n == cp.n
    return np.ascontiguousarray(np.concatenate(cp.blocks, axis=1))


def _kmaj(w):
    K, N = w.shape
    return w.reshape(K // 128, 128, N).transpose(1, 0, 2)


def pack_weights(inp):
    out = {}
    a = inp["ada_w"].reshape(DEPTH, 8, 128, 12, 512).transpose(0, 3, 2, 1, 4)
    out["adap"] = np.ascontiguousarray(a).reshape(DEPTH, 12, 128, 4096)
    w = inp["cv_pw1_w"].reshape(2, 8, 128, 2, 4, 2, 128)
    w = w.transpose(0, 4, 2, 1, 3, 5, 6)
    out["pw1p"] = np.ascontiguousarray(w).reshape(2, 4, 128, 4096)
    w = inp["cv_pw2_w"].reshape(2, 8, 128, 2, 512).transpose(0, 3, 2, 1, 4)
    out["pw2p"] = np.ascontiguousarray(w).reshape(2, 2, 128, 4096)
    w = inp["da_wqkv"].reshape(2, 8, 128, 3, 8, 128)
    w = w.transpose(0, 4, 2, 1, 3, 5)
    out["wqkvp"] = np.ascontiguousarray(w).reshape(2, 8, 128, 3072)
    out["wop"] = np.ascontiguousarray(inp["da_wo"].reshape(2, 8, 128, 1024))
    w = inp["ffn_w_up"].reshape(DEPTH, 8, 128, 2, 11, 2, 128)
    w = w.transpose(0, 4, 2, 1, 3, 5, 6)
    out["upp"] = np.ascontiguousarray(w).reshape(DEPTH, 11, 128, 4096)
    w = inp["ffn_w_down"].reshape(DEPTH, 22, 128, 8, 128).transpose(0, 3, 2, 1, 4)
    out["dnp"] = np.ascontiguousarray(w).reshape(DEPTH, 8, 128, 2816)
    return out


def make_consts():
    cm = np.zeros((128, 7, 128), np.float32)
    cm[:, 0] = np.eye(128)
    cm[:, 1] = 1.0 / 1024.0
    blk = (np.arange(128)[:, None] // 64) == (np.arange(128)[None, :] // 64)
    cm[:, 2] = blk / 64.0
    cm[:, 3] = 1.0 / 128.0
    R = np.zeros((128, 128), np.float32)
    for m in range(128):
        if (m % 32) < 16:
            R[m + 16, m] = -1.0
        else:
            R[m - 16, m] = 1.0
    cm[:, 4] = R
    cm[:, 5] = 1.0
    ind = np.zeros((128, 128), np.float32)
    ind[:64, 0] = 1.0 / 64.0
    ind[64:, 1] = 1.0 / 64.0
    cm[:, 6] = ind
    t = np.arange(SEQ)
    row = (t // 64).astype(np.float32)
    colp = (t % 64).astype(np.float32)
    half = 32
    inv = (10000.0 ** (-np.arange(0, half, 2, dtype=np.float32) / half)).astype(np.float32)
    ang_r = row[:, None] * inv
    ang_c = colp[:, None] * inv
    ang = np.concatenate([ang_r, ang_r, ang_c, ang_c], axis=-1).astype(np.float32)
    cos = np.ones((128, NTOK), np.float32)
    sin = np.zeros((128, NTOK), np.float32)
    cos[:, CTX:] = np.tile(np.cos(ang).T, (2, 1))
    sin[:, CTX:] = np.tile(np.sin(ang).T, (2, 1))
    return cm.reshape(128, 7 * 128), np.ascontiguousarray(np.stack([cos, sin], axis=1)).reshape(128, 2 * NTOK)


class WStream:
    NSLOT = 3
    AHEAD = 2

    def __init__(self, P, slots_ap, known=None):
        self.P = P
        self.slots_ap = slots_ap
        self.known = known
        self.req = []
        self.res = [Res() for _ in range(self.NSLOT)]
        self.issued = 0
        self.i = 0

    def _issue(self, idx):
        src, nel = self.known[idx]
        slot = idx % self.NSLOT
        dst = self.slots_ap[:, slot * 4096: slot * 4096 + nel]
        self.P.dma("pool", lambda e, d=dst, s=src: e.dma_start(out=d, in_=s, max_dma_last_dim=4096), w=[self.res[slot]])

    def next(self, src, nel):
        idx = self.i
        self.i += 1
        self.req.append((src, nel))
        if self.known is None:
            slot = idx % self.NSLOT
            return self.slots_ap[:, slot * 4096: slot * 4096 + nel], self.res[slot]
        while self.issued < min(len(self.known), idx + 1 + self.AHEAD):
            self._issue(self.issued)
            self.issued += 1
        slot = idx % self.NSLOT
        return self.slots_ap[:, slot * 4096: slot * 4096 + nel], self.res[slot]


def build_program(nb=4, nlayers=DEPTH, debug=False, dry_known=None, stop_mixer=False):
    nc = bass.Bass("TRN2", target_bir_lowering=False)
    coff, ncol = col_layout()

    dr = {}
    dr["xT"] = nc.dram_tensor("xT", [nb, 128, NCH, SEQ], F32, kind="ExternalInput").ap()
    dr["ctxT"] = nc.dram_tensor("ctxT", [nb, 128, NCH, CTX], F32, kind="ExternalInput").ap()
    dr["cols"] = nc.dram_tensor("cols", [128, ncol], F32, kind="ExternalInput").ap()
    dr["cmat"] = nc.dram_tensor("cmat", [128, 7 * 128], F32, kind="ExternalInput").ap()
    dr["rope"] = nc.dram_tensor("rope", [128, 2 * NTOK], F32, kind="ExternalInput").ap()
    dr["adap"] = nc.dram_tensor("adap", [DEPTH, 12, 128, 4096], F32, kind="ExternalInput").ap()
    dr["pw1p"] = nc.dram_tensor("pw1p", [2, 4, 128, 4096], F32, kind="ExternalInput").ap()
    dr["pw2p"] = nc.dram_tensor("pw2p", [2, 2, 128, 4096], F32, kind="ExternalInput").ap()
    dr["wqkvp"] = nc.dram_tensor("wqkvp", [2, 8, 128, 3072], F32, kind="ExternalInput").ap()
    dr["wop"] = nc.dram_tensor("wop", [2, 8, 128, 1024], F32, kind="ExternalInput").ap()
    dr["upp"] = nc.dram_tensor("upp", [DEPTH, 11, 128, 4096], F32, kind="ExternalInput").ap()
    dr["dnp"] = nc.dram_tensor("dnp", [DEPTH, 8, 128, 2816], F32, kind="ExternalInput").ap()
    dr["out"] = nc.dram_tensor("out", [nb, 128, NCH, SEQ], F32, kind="ExternalOutput").ap()
    if debug:
        dr["octx"] = nc.dram_tensor("octx", [nb, 128, NCH, CTX], F32, kind="ExternalOutput").ap()
        dr["dscr"] = nc.dram_tensor("dscr", [128, 14336], F32, kind="ExternalOutput").ap()

    with ExitStack() as es:
        def sb(name, shape, dt):
            return es.enter_context(nc.sbuf_tensor(name, shape, dt))

        xres_t = sb("xres", [128, NCH * NTOK], F32)
        hbuf_t = sb("hbuf", [128, NCH * NTOK], BF16)
        wsl_t = sb("wsl", [128, 3 * 4096], BF16)
        cols_t = sb("colsb", [128, ncol], F32)
        cm_t = sb("cmb", [128, 7 * 128], BF16)
        modT_t = sb("modT", [128, DEPTH * 48 * NSTREAM], F32)
        der_t = sb("der", [128, 1280], F32)
        SCR_W = 14336
        scr_t = sb("scr", [128, SCR_W], F32)
        banks = [es.enter_context(nc.psum_tensor("bank%d" % i, [128, 512], F32)) for i in range(8)]

        xres = xres_t[:, :].rearrange("p (c t) -> p c t", c=NCH)
        hbuf = hbuf_t[:, :].rearrange("p (c t) -> p c t", c=NCH)
        wsl = wsl_t[:, :]
        cols = cols_t[:, :]
        cm = cm_t[:, :]
        scr32 = scr_t[:, :]
        scr16 = scr_t[:, :].bitcast(BF16)
        IDENT, ONES1024, BLK64, ONES128, RROT, ONES1, IND2 = [cm[:, i * 128:(i + 1) * 128] for i in range(7)]

        def C(name, j=0, n=1):
            o, wd = coff[name]
            assert j + n <= wd
            return cols[:, o + j: o + j + n]

        modT = modT_t[:, :].rearrange("p (l j s) -> p l j s", l=DEPTH, j=48)
        der = der_t[:, :]
        DA_M = der[:, 0:160].rearrange("p (l c s) -> p l c s", l=DEPTH, c=8)
        DA_F = der[:, 160:320].rearrange("p (l c s) -> p l c s", l=DEPTH, c=8)
        DGB = der[:, 320:480].rearrange("p (l c s) -> p l c s", l=DEPTH, c=8)
        HB1 = der[:, 480:512].rearrange("p (j c) -> p j c", j=2)
        LAM = der[:, 512:520]
        TMP = der[:, 520:1024]
        SCT = der[:, 1024:1064]

        def build(P, WS):
            xr = [[Res() for _ in range(9)] for _ in range(NCH)]
            hr = [[Res() for _ in range(9)] for _ in range(NCH)]
            bres = [Res() for _ in range(8)]
            r_cols, r_cm, r_mod, r_der = Res(), Res(), Res(), Res()
            bank_i = [0]

            def XR(c, t0, T):
                return [xr[c][k] for k in range(t0 // 256, (t0 + T + 255) // 256)]

            def HR(c, t0, T):
                return [hr[c][k] for k in range(t0 // 256, (t0 + T + 255) // 256)]

            def HRall(t0, T):
                return [r for c in range(NCH) for r in HR(c, t0, T)]

            def nbank(pool=(0, 1, 2, 3, 4, 5, 6, 7)):
                b = pool[bank_i[0] % len(pool)]
                bank_i[0] += 1
                return b

            def mm(out, lhsT, rhs, start, stop, r, w):
                return P.op("pe", lambda e: e.matmul(out, lhsT=lhsT, rhs=rhs, start=start, stop=stop), r=r, w=w)

            def act(out, in_, func, r, w, bias=0.0, scale=1.0):
                return P.op("act", lambda e: e.activation(out=out, in_=in_, func=func, bias=bias, scale=scale), r=r, w=w)

            def tt(eng, out, in0, in1, op, r, w):
                return P.op(eng, lambda e: e.tensor_tensor(out=out, in0=in0, in1=in1, op=op), r=r, w=w)

            def ts(eng, out, in0, s1, s2, op0, op1, r, w):
                return P.op(eng, lambda e: e.tensor_scalar(out=out, in0=in0, scalar1=s1, scalar2=s2, op0=op0, op1=op1), r=r, w=w)

            def stt(out, in0, scalar, in1, op0, op1, r, w):
                return P.op("dve", lambda e: e.scalar_tensor_tensor(out=out, in0=in0, scalar=scalar, in1=in1, op0=op0, op1=op1), r=r, w=w)

            def recip(out, in_, r, w):
                return P.op("dve", lambda e: e.reciprocal(out=out, in_=in_), r=r, w=w)

            def rsqrt_act(out, in_, r_in, r_out, tmp, r_tmp, post_scale=1.0):
                act(tmp, in_, AF.Ln, r=r_in, w=r_tmp, bias=EPSB)
                b = 0.0 if post_scale == 1.0 else None
                if b is None:
                    act(out, tmp, AF.Exp, r=r_tmp, w=r_out, scale=-0.5, bias=LNSC)
                else:
                    act(out, tmp, AF.Exp, r=r_tmp, w=r_out, scale=-0.5)

            P.dma("sp", lambda e: e.dma_start(out=cols, in_=dr["cols"][:, :]), w=[r_cols])
            P.dma("pool", lambda e: e.dma_start(out=cm, in_=dr["cmat"][:, :], max_dma_last_dim=3584), w=[r_cm])
            CONSTC = der[:, 1064:1072]
            P.op("dve", lambda e: e.memset(CONSTC[:, 0:1], EPS), w=[r_der])
            P.op("dve", lambda e: e.memset(CONSTC[:, 1:2], math.log(DA_SCALE)), w=[r_der])
            EPSB_ = CONSTC[:, 0:1]
            LNSC_ = CONSTC[:, 1:2]

            act(SCT, C("cT", 0, 40), AF.Silu, r=[r_cols], w=[r_der])
            aslot = [scr32[:, 0:4096], scr32[:, 4096:8192]]
            ares = [Res(), Res()]
            pi = 0
            for l in range(nlayers):
                mb = nbank()
                for j12 in range(12):
                    s = pi % 2
                    pi += 1
                    src = dr["adap"][l, j12]
                    P.dma("sp", lambda e, d=aslot[s], s_=src: e.dma_start(out=d, in_=s_), w=[ares[s]])
                    wv = aslot[s].rearrange("p (k n) -> p k n", k=8)
                    for jj in range(4):
                        j = j12 * 4 + jj
                        for kc in range(8):
                            mm(banks[mb][:, j * 8: j * 8 + NSTREAM], wv[:, kc, jj * 128:(jj + 1) * 128],
                               SCT[:, kc * NSTREAM:(kc + 1) * NSTREAM], kc == 0, kc == 7,
                               r=[ares[s], r_der], w=[bres[mb]])
                o, _ = coff["ada_b"]
                bias = cols[:, o + l * 48: o + (l + 1) * 48].unsqueeze(2).broadcast_to([128, 48, NSTREAM])
                src = banks[mb][:, 0:384].rearrange("p (j e) -> p j e", e=8)[:, :, 0:NSTREAM]
                tt("dve", modT[:, l], src, bias, ALU.add, r=[bres[mb], r_cols], w=[r_mod])
                for which, dst, gname in ((1, DA_M, "mixg%d" % l), (4, DA_F, "ffng%d" % l)):
                    g = C(gname, 0, 8).unsqueeze(2).broadcast_to([128, 8, NSTREAM])
                    P.op("dve", lambda e, d=dst[:, l], s_=modT[:, l, which * 8:(which + 1) * 8, :], g=g:
                         e.scalar_tensor_tensor(out=d, in0=s_, scalar=1.0, in1=g, op0=ALU.add, op1=ALU.mult),
                         r=[r_mod, r_cols], w=[r_der])
                if l % 2 == 0:
                    jv = l // 2
                    b2 = C("pw2b%d" % jv, 0, 8).unsqueeze(2).broadcast_to([128, 8, NSTREAM])
                    tt("dve", DGB[:, l], modT[:, l, 16:24, :], b2, ALU.mult, r=[r_mod, r_cols], w=[r_der])
                    ts("dve", HB1[:, jv], C("pw1b%d" % jv, 0, 16), 0.5, None, ALU.mult, ALU.bypass, r=[r_cols], w=[r_der])
                else:
                    jv = l // 2
                    lam_init = 0.8 - 0.6 * math.exp(-0.3 * l)
                    L4 = LAM[:, jv * 4: jv * 4 + 4]
                    for k, (a, b) in enumerate((("lq1", "lk1"), ("lq2", "lk2"))):
                        tt("dve", TMP[:, 0:64], C("%s%d" % (a, jv), 0, 64), C("%s%d" % (b, jv), 0, 64), ALU.mult,
                           r=[r_cols, r_der], w=[r_der])
                        P.op("dve", lambda e, o_=TMP[:, 64 + k: 65 + k]: e.tensor_reduce(out=o_, in_=TMP[:, 0:64],
                             axis=mybir.AxisListType.X, op=ALU.add), r=[r_der], w=[r_der])
                    act(TMP[:, 66:68], TMP[:, 64:66], AF.Exp, r=[r_der], w=[r_der])
                    tt("dve", L4[:, 0:1], TMP[:, 66:67], TMP[:, 67:68], ALU.subtract, r=[r_der], w=[r_der])
                    ts("dve", L4[:, 0:1], L4[:, 0:1], lam_init, None, ALU.add, ALU.bypass, r=[r_der], w=[r_der])
                    ts("dve", L4[:, 1:2], L4[:, 0:1], -1.0, None, ALU.mult, ALU.bypass, r=[r_der], w=[r_der])
                    ts("dve", L4[:, 2:3], C("subg%d" % jv), 1.0 - lam_init, None, ALU.mult, ALU.bypass, r=[r_cols, r_der], w=[r_der])
            P.barrier()

            EPSB, LNSC = EPSB_, LNSC_

            def load_x(b):
                for c in range(NCH):
                    P.dma("sp", lambda e, c=c: e.dma_start(out=xres[:, c, CTX:NTOK], in_=dr["xT"][b, :, c, :]), w=XR(c, CTX, SEQ))
                for c in range(NCH):
                    P.dma("sp", lambda e, c=c: e.dma_start(out=xres[:, c, 0:CTX], in_=dr["ctxT"][b, :, c, :]), w=XR(c, 0, CTX))

            def store_x(b):
                for c in range(NCH):
                    P.dma("sp", lambda e, c=c: e.dma_start(out=dr["out"][b, :, c, :], in_=xres[:, c, CTX:NTOK]), r=XR(c, CTX, SEQ))
                if debug:
                    for c in range(NCH):
                        P.dma("sp", lambda e, c=c: e.dma_start(out=dr["octx"][b, :, c, :], in_=xres[:, c, 0:CTX]), r=XR(c, 0, CTX))

            def tiles_for(with_ctx):
                t = [(0, CTX)] if with_ctx else []
                return t + [(CTX + 512 * i, 512) for i in range(4)]

            def stream_of(b, t0):
                return 4 if t0 < CTX else b

            def norm_mod(b, l, which, with_ctx):
                P.barrier()
                A = DA_M if which == "m" else DA_F
                shj = 0 if which == "m" else 24
                sq = [scr16[:, i * 4096:(i + 1) * 4096].rearrange("p (c t) -> p c t", c=8) for i in range(2)]
                r_sq = [Res(), Res()]
                rstd = scr32[:, 4096:4096 + NTOK]
                r_rstd = [Res() for _ in range(9)]
                lnt = [scr32[:, 6400 + i * 512: 6400 + (i + 1) * 512] for i in range(2)]
                r_lnt = [Res(), Res()]
                tmp = [scr32[:, 7424 + i * 512: 7424 + (i + 1) * 512] for i in range(4)]
                r_tmp = [Res() for _ in range(4)]
                ti = 0
                for ii, (t0, T) in enumerate(tiles_for(with_ctx)):
                    s = stream_of(b, t0)
                    q = ii % 2
                    xin = [r for c in range(NCH) for r in XR(c, t0, T)]
                    act(sq[q][:, :, 0:T], xres[:, :, t0:t0 + T], AF.Square, r=xin, w=[r_sq[q]])
                    bk = nbank()
                    for c in range(NCH):
                        mm(banks[bk][:, 0:T], ONES1024, sq[q][:, c, 0:T], c == 0, c == NCH - 1, r=[r_sq[q], r_cm], w=[bres[bk]])
                    rr = r_rstd[t0 // 256:(t0 + T) // 256]
                    rsqrt_act(rstd[:, t0:t0 + T], banks[bk][:, 0:T], [bres[bk]], rr, lnt[q][:, 0:T], [r_lnt[q]])
                    for c in range(NCH):
                        k = ti % 4
                        ti += 1
                        eng = "pool" if (c % 2 == 0) else "dve"
                        tt(eng, tmp[k][:, 0:T], xres[:, c, t0:t0 + T], rstd[:, t0:t0 + T], ALU.mult,
                           r=XR(c, t0, T) + rr, w=[r_tmp[k]])
                        act(hbuf[:, c, t0:t0 + T], tmp[k][:, 0:T], AF.Identity, r=[r_tmp[k], r_der, r_mod], w=HR(c, t0, T),
                            scale=A[:, l, c, s:s + 1], bias=modT[:, l, shj + c, s:s + 1])

            def conv_mixer(b, l, with_ctx):
                P.barrier()
                jv = l // 2
                diag = [scr16[:, i * 3968:(i + 1) * 3968].rearrange("p (k m) -> p k m", k=31) for i in range(2)]
                r_diag = [Res(), Res()]
                up_l = [scr16[:, 7936 + i * 752: 7936 + (i + 1) * 752] for i in range(3)]
                up_c = [scr16[:, 10192 + i * 288: 10192 + (i + 1) * 288] for i in range(3)]
                r_up = [Res() for _ in range(3)]
                vbuf = scr32[:, 5528:5528 + 4096].rearrange("p (c t) -> p c t", c=8)
                r_vb = [Res() for _ in range(8)]
                zb = scr16[:, 19248:19248 + 4096].rearrange("p (c t) -> p c t", c=8)
                r_zb = [Res() for _ in range(8)]
                st16 = [scr16[:, 23344 + i * 512: 23344 + (i + 1) * 512] for i in range(4)]
                r_st = [Res() for _ in range(4)]
                f32t = [scr32[:, 12696 + i * 512: 12696 + (i + 1) * 512] for i in range(3)]
                r_ft = [Res() for _ in range(3)]
                for i in range(3):
                    P.op("pool", lambda e, a=up_l[i]: e.memset(a, 0.0), w=[r_up[i]])
                zc = [False]
                ui = 0
                for (t0, T) in tiles_for(with_ctx):
                    s = stream_of(b, t0)
                    isc = t0 < CTX
                    if isc:
                        for i in range(3):
                            P.op("pool", lambda e, a=up_c[i]: e.memset(a, 0.0), w=[r_up[i]])
                    R_, W_ = (1, 256) if isc else (8, 64)
                    hin = HRall(t0, T)
                    MB, QB = 6, 7
                    for jp in range(4):
                        wap, wres = WS.next(dr["pw1p"][jv, jp], 4096)
                        wv = wap.rearrange("p (k n) -> p k n", k=8)
                        for cc in range(2):
                            c = 2 * jp + cc
                            ba, bg = nbank((0, 1, 2, 3, 4, 5)), nbank((0, 1, 2, 3, 4, 5))
                            for kc in range(NCH):
                                mm(banks[ba][:, 0:T], wv[:, kc, cc * 128:(cc + 1) * 128], hbuf[:, kc, t0:t0 + T], kc == 0, kc == 7,
                                   r=[wres] + HR(kc, t0, T), w=[bres[ba]])
                            for kc in range(NCH):
                                mm(banks[bg][:, 0:T], wv[:, kc, (2 + cc) * 128:(3 + cc) * 128], hbuf[:, kc, t0:t0 + T], kc == 0, kc == 7,
                                   r=[wres] + HR(kc, t0, T), w=[bres[bg]])
                            fa, fb = 0, 1
                            act(f32t[fa][:, 0:T], banks[bg][:, 0:T], AF.Tanh, r=[bres[bg], r_der], w=[r_ft[fa]],
                                scale=0.5, bias=HB1[:, jv, 8 + c: 9 + c])
                            act(f32t[fb][:, 0:T], banks[ba][:, 0:T], AF.Identity, r=[bres[ba], r_der], w=[r_ft[fb]],
                                scale=0.5, bias=HB1[:, jv, c: c + 1])
                            u = ui % 3
                            ui += 1
                            if isc:
                                upv = up_c[u][:, 0:286].rearrange("p (r w) -> p r w", r=1)
                            else:
                                upv = up_l[u][:, 0:752].rearrange("p (r w) -> p r w", r=8)
                            stt(upv[:, :, 15:15 + W_], f32t[fa][:, 0:T].rearrange("p (r w) -> p r w", r=R_), 1.0,
                                f32t[fb][:, 0:T].rearrange("p (r w) -> p r w", r=R_), ALU.add, ALU.mult,
                                r=[r_ft[fa], r_ft[fb]], w=[r_up[u]])
                            dg = c % 2
                            for k in range(31):
                                P.op("pool", lambda e, o_=diag[dg][:, k, :], s_=C("dww%d" % jv, c * 31 + k):
                                     e.tensor_scalar(out=o_, in0=IDENT, scalar1=s_, scalar2=0.0, op0=ALU.mult, op1=ALU.add),
                                     r=[r_cols, r_cm], w=[r_diag[dg]])
                            bv = nbank((0, 1, 2, 3, 4, 5))
                            for k in range(31):
                                mm(banks[bv][:, 0:T].rearrange("p (r w) -> p r w", r=R_), diag[dg][:, k, :], upv[:, :, k:k + W_],
                                   k == 0, k == 30, r=[r_diag[dg], r_up[u]], w=[bres[bv]])
                            act(vbuf[:, c, 0:T], banks[bv][:, 0:T], AF.Identity, r=[bres[bv], r_cols], w=[r_vb[c]],
                                bias=C("dwb%d" % jv, c))
                            s0, s1 = (2 * c) % 4, (2 * c + 1) % 4
                            act(st16[s0][:, 0:T], banks[bv][:, 0:T], AF.Identity, r=[bres[bv], r_cols], w=[r_st[s0]],
                                bias=C("dwb%d" % jv, c))
                            act(st16[s1][:, 0:T], banks[bv][:, 0:T], AF.Square, r=[bres[bv], r_cols], w=[r_st[s1]],
                                bias=C("dwb%d" % jv, c))
                            mm(banks[MB][:, 0:T], ONES1024, st16[s0][:, 0:T], c == 0, c == 7, r=[r_st[s0], r_cm], w=[bres[MB]])
                            mm(banks[QB][:, 0:T], ONES1024, st16[s1][:, 0:T], c == 0, c == 7, r=[r_st[s1], r_cm], w=[bres[QB]])
                    mu, var, rs = f32t[0], f32t[1], f32t[2]
                    P.op("dve", lambda e, o_=mu[:, 0:T], i_=banks[MB][:, 0:T]: e.tensor_copy(out=o_, in_=i_), r=[bres[MB]], w=[r_ft[0]])
                    tt("dve", var[:, 0:T], mu[:, 0:T], mu[:, 0:T], ALU.mult, r=[r_ft[0]], w=[r_ft[1]])
                    tt("dve", var[:, 0:T], banks[QB][:, 0:T], var[:, 0:T], ALU.subtract, r=[bres[QB], r_ft[1]], w=[r_ft[1]])
                    rsqrt_act(rs[:, 0:T], var[:, 0:T], [r_ft[1]], [r_ft[2]], var[:, 0:T], [r_ft[1]])
                    for c in range(NCH):
                        k0 = c % 4
                        dtmp = st16
                        tt("dve", vbuf[:, c, 0:T], vbuf[:, c, 0:T], mu[:, 0:T], ALU.subtract, r=[r_vb[c], r_ft[0]], w=[r_vb[c]])
                        tt("pool", vbuf[:, c, 0:T], vbuf[:, c, 0:T], rs[:, 0:T], ALU.mult, r=[r_vb[c], r_ft[2]], w=[r_vb[c]])
                        act(zb[:, c, 0:T], vbuf[:, c, 0:T], AF.Silu, r=[r_vb[c], r_cols], w=[r_zb[c]],
                            scale=C("lng%d" % jv, c), bias=C("lnb%d" % jv, c))
                    yt = [scr16[:, 23344 + i * 1024: 23344 + (i + 1) * 1024].bitcast(F32) for i in range(2)]
                    for jp in range(2):
                        wap, wres = WS.next(dr["pw2p"][jv, jp], 4096)
                        wv = wap.rearrange("p (k n) -> p k n", k=8)
                        for cc in range(4):
                            n = 4 * jp + cc
                            by = nbank((0, 1, 2, 3, 4, 5))
                            for kc in range(NCH):
                                mm(banks[by][:, 0:T], wv[:, kc, cc * 128:(cc + 1) * 128], zb[:, kc, 0:T], kc == 0, kc == 7,
                                   r=[wres, r_zb[kc]], w=[bres[by]])
                            y = n % 2
                            yr = [r_st[2 * y], r_st[2 * y + 1]]
                            act(yt[y][:, 0:T], banks[by][:, 0:T], AF.Identity, r=[bres[by], r_mod, r_der], w=yr,
                                scale=modT[:, l, 16 + n, s:s + 1], bias=DGB[:, l, n, s:s + 1])
                            tt("pool", xres[:, n, t0:t0 + T], xres[:, n, t0:t0 + T], yt[y][:, 0:T], ALU.add,
                               r=yr + XR(n, t0, T), w=XR(n, t0, T))

            def ffn(b, l, with_ctx):
                P.barrier()
                if with_ctx:
                    groups = [[(0, 256), (256, 512)], [(768, 512), (1280, 256)], [(1536, 512), (2048, 256)]]
                else:
                    groups = [[(256, 512), (768, 256)], [(1024, 512), (1536, 256)], [(1792, 512)]]
                gb = scr16[:, 0:NFC * 768].rearrange("p (c t) -> p c t", c=NFC)
                r_g = [[Res(), Res()] for _ in range(NFC)]
                ca = [scr32[:, 8448 + i * 512: 8448 + (i + 1) * 512] for i in range(4)]
                cv = [scr32[:, 10496 + i * 512: 10496 + (i + 1) * 512] for i in range(4)]
                yt = [scr32[:, 12544 + i * 512: 12544 + (i + 1) * 512] for i in range(3)]
                r_ca = [Res() for _ in range(4)]
                r_cv = [Res() for _ in range(4)]
                r_yt = [Res() for _ in range(3)]
                ki = 0
                yi = 0
                for grp in groups:
                    for jp in range(11):
                        wap, wres = WS.next(dr["upp"][l, jp], 4096)
                        wv = wap.rearrange("p (k n) -> p k n", k=8)
                        for cc in range(2):
                            ch = 2 * jp + cc
                            g0 = 0
                            for gi, (t0, T) in enumerate(grp):
                                isc = t0 < CTX
                                R_, W_ = (1, 256) if isc else (T // 64, 64)
                                ba, bv = nbank(), nbank()
                                for kc in range(NCH):
                                    mm(banks[ba][:, 0:T], wv[:, kc, cc * 128:(cc + 1) * 128], hbuf[:, kc, t0:t0 + T], kc == 0, kc == 7,
                                       r=[wres] + HR(kc, t0, T), w=[bres[ba]])
                                for kc in range(NCH):
                                    mm(banks[bv][:, 0:T], wv[:, kc, (2 + cc) * 128:(3 + cc) * 128], hbuf[:, kc, t0:t0 + T], kc == 0, kc == 7,
                                       r=[wres] + HR(kc, t0, T), w=[bres[bv]])
                                k = ki % 4
                                ki += 1
                                for (bk, dst, rdst, chn) in ((ba, ca[k], r_ca[k], ch), (bv, cv[k], r_cv[k], NFC + ch)):
                                    act(dst[:, 0:T], banks[bk][:, 0:T], AF.Identity, r=[bres[bk], r_cols], w=[rdst],
                                        scale=C("fdw%d" % l, chn * 3 + 1), bias=C("fdb%d" % l, chn))
                                    d3 = dst[:, 0:T].rearrange("p (r w) -> p r w", r=R_)
                                    p3 = banks[bk][:, 0:T].rearrange("p (r w) -> p r w", r=R_)
                                    stt(d3[:, :, 1:W_], p3[:, :, 0:W_ - 1], C("fdw%d" % l, chn * 3 + 0), d3[:, :, 1:W_],
                                        ALU.mult, ALU.add, r=[bres[bk], r_cols, rdst], w=[rdst])
                                    stt(d3[:, :, 0:W_ - 1], p3[:, :, 1:W_], C("fdw%d" % l, chn * 3 + 2), d3[:, :, 0:W_ - 1],
                                        ALU.mult, ALU.add, r=[bres[bk], r_cols, rdst], w=[rdst])
                                act(ca[k][:, 0:T], ca[k][:, 0:T], AF.Silu, r=[r_ca[k]], w=[r_ca[k]])
                                tt("pool", gb[:, ch, g0:g0 + T], ca[k][:, 0:T], cv[k][:, 0:T], ALU.mult,
                                   r=[r_ca[k], r_cv[k]], w=[r_g[ch][gi]])
                                g0 += T
                    for n in range(NCH):
                        wap, wres = WS.next(dr["dnp"][l, n], 2816)
                        wv = wap.rearrange("p (k m) -> p k m", k=NFC)
                        g0 = 0
                        for gi, (t0, T) in enumerate(grp):
                            s = stream_of(b, t0)
                            by = nbank()
                            for kc in range(NFC):
                                mm(banks[by][:, 0:T], wv[:, kc, :], gb[:, kc, g0:g0 + T], kc == 0, kc == NFC - 1,
                                   r=[wres, r_g[kc][gi]], w=[bres[by]])
                            y = yi % 3
                            yi += 1
                            act(yt[y][:, 0:T], banks[by][:, 0:T], AF.Identity, r=[bres[by], r_mod], w=[r_yt[y]],
                                scale=modT[:, l, 40 + n, s:s + 1])
                            tt("pool", xres[:, n, t0:t0 + T], xres[:, n, t0:t0 + T], yt[y][:, 0:T], ALU.add,
                               r=[r_yt[y]] + XR(n, t0, T), w=XR(n, t0, T))
                            g0 += T

            def attention(b, l, need_ctx):
                P.barrier()
                jv = l // 2
                L4 = LAM[:, jv * 4: jv * 4 + 4]
                cosT = scr32[:, 0:NTOK]
                sinT = scr32[:, NTOK:2 * NTOK]
                r_rope = Res()
                P.dma("sp", lambda e: e.dma_start(out=scr32[:, 0:2 * NTOK], in_=dr["rope"][:, :]), w=[r_rope])
                qT0 = scr16[:, 9216:9216 + NTOK]
                qT1 = scr16[:, 11520:11520 + NTOK]
                kT = scr16[:, 13824:13824 + NTOK]
                Vb = scr16[:, 16128:16128 + 18 * 128].rearrange("p (k e) -> p k e", k=18)
                r_q = [Res() for _ in range(9)]
                r_k = [Res() for _ in range(9)]
                r_v = [Res() for _ in range(9)]
                pt = [scr16[:, 18432 + i * 512: 18432 + (i + 1) * 512] for i in range(4)]
                r_pt = [Res() for _ in range(4)]
                NFT = 6
                f32t = [scr32[:, 10240 + i * 512: 10240 + (i + 1) * 512] for i in range(NFT)]
                r_ft = [Res() for _ in range(NFT)]
                b16t = [scr16[:, 26624 + i * 512: 26624 + (i + 1) * 512] for i in range(4)]
                r_bt = [Res() for _ in range(4)]
                skc = TMP[:, 100:136]
                sktmp = TMP[:, 140:176]
                r_sk = Res()
                P.op("pool", lambda e: e.memset(qT0[64:128, :], 0.0), w=r_q)
                P.op("pool", lambda e: e.memset(qT1[0:64, :], 0.0), w=r_q)
                fi = [0]
                bi = [0]

                def ftmp():
                    k = fi[0] % NFT
                    fi[0] += 1
                    return f32t[k], r_ft[k]

                def btmp():
                    k = bi[0] % 4
                    bi[0] += 1
                    return b16t[k], r_bt[k]

                tl_all = tiles_for(True)
                for hd in range(8):
                    wap, wres = WS.next(dr["wqkvp"][jv, hd], 3072)
                    wv = wap.rearrange("p (k n) -> p k n", k=8)
                    skb = 7
                    PB = (0, 1, 2, 3, 4, 5, 6)
                    for which in (0, 1):
                        gcol = C("qng%d" % jv) if which == 0 else C("kng%d" % jv)
                        for (t0, T) in tl_all:
                            if which == 0 and t0 < CTX and not need_ctx:
                                continue
                            bp = nbank(PB)
                            for kc in range(NCH):
                                mm(banks[bp][:, 0:T], wv[:, kc, which * 128:(which + 1) * 128], hbuf[:, kc, t0:t0 + T], kc == 0, kc == 7,
                                   r=[wres] + HR(kc, t0, T), w=[bres[bp]])
                            gb16, r_gb = btmp()
                            sq16, r_sq = btmp()
                            g32, r_g32 = ftmp()
                            act(gb16[:, 0:T], banks[bp][:, 0:T], AF.Identity, r=[bres[bp], r_cols], w=[r_gb], scale=gcol)
                            act(g32[:, 0:T], banks[bp][:, 0:T], AF.Identity, r=[bres[bp], r_cols], w=[r_g32], scale=gcol)
                            act(sq16[:, 0:T], banks[bp][:, 0:T], AF.Square, r=[bres[bp]], w=[r_sq])
                            br = nbank(PB)
                            mm(banks[br][:, 0:T], RROT, gb16[:, 0:T], True, True, r=[r_gb, r_cm], w=[bres[br]])
                            t1, r_t1 = ftmp()
                            t2, r_t2 = ftmp()
                            tt("dve", t1[:, 0:T], g32[:, 0:T], cosT[:, t0:t0 + T], ALU.mult, r=[r_g32, r_rope], w=[r_t1])
                            tt("dve", t2[:, 0:T], banks[br][:, 0:T], sinT[:, t0:t0 + T], ALU.mult, r=[bres[br], r_rope], w=[r_t2])
                            rr = (r_q if which == 0 else r_k)[t0 // 256:(t0 + T) // 256]
                            if which == 0:
                                bs = nbank(PB)
                                mm(banks[bs][:, 0:T], BLK64, sq16[:, 0:T], True, True, r=[r_sq, r_cm], w=[bres[bs]])
                                rq, r_rq = ftmp()
                                lt, r_lt = ftmp()
                                rsqrt_act(rq[:, 0:T], banks[bs][:, 0:T], [bres[bs]], [r_rq], lt[:, 0:T], [r_lt])
                                tt("pool", t1[:, 0:T], t1[:, 0:T], t2[:, 0:T], ALU.add, r=[r_t1, r_t2], w=[r_t1])
                                tt("dve", qT0[0:64, t0:t0 + T], t1[0:64, 0:T], rq[0:64, 0:T], ALU.mult, r=[r_t1, r_rq], w=rr)
                                tt("dve", qT1[64:128, t0:t0 + T], t1[64:128, 0:T], rq[64:128, 0:T], ALU.mult, r=[r_t1, r_rq], w=rr)
                            else:
                                tt("pool", kT[:, t0:t0 + T], t1[:, 0:T], t2[:, 0:T], ALU.add, r=[r_t1, r_t2], w=rr)
                                for kk in range(T // 128):
                                    kt = t0 // 128 + kk
                                    mm(banks[skb][:, kt * 2: kt * 2 + 2], sq16[:, kk * 128:(kk + 1) * 128], IND2[:, 0:2], True, True,
                                       r=[r_sq, r_cm], w=[bres[skb]])
                    rsqrt_act(skc, banks[skb][:, 0:36], [bres[skb]], [r_sk], sktmp, [r_sk], post_scale=DA_SCALE)
                    for g4 in range(5):
                        kts = list(range(g4 * 4, min(18, g4 * 4 + 4)))
                        bv = nbank(PB)
                        for i, kt in enumerate(kts):
                            for kc in range(NCH):
                                mm(banks[bv][:, i * 128:(i + 1) * 128], hbuf[:, kc, kt * 128:(kt + 1) * 128], wv[:, kc, 256:384], kc == 0, kc == 7,
                                   r=[wres] + HR(kc, kt * 128, 128), w=[bres[bv]])
                        n_ = len(kts)
                        act(Vb[:, kts[0]:kts[0] + n_, :], banks[bv][:, 0:n_ * 128].rearrange("p (k e) -> p k e", k=n_), AF.Identity,
                            r=[bres[bv]], w=r_v[kts[0] // 2:(kts[-1]) // 2 + 1])
                    woap, wores = WS.next(dr["wop"][jv, hd], 1024)
                    qsets = [(CTX + 512 * i, 512, list(range(18)), b) for i in range(4)]
                    if need_ctx:
                        qsets.append((0, CTX, [0, 1], 4))
                    pti = 0
                    for (q0, TQ, keys, s) in qsets:
                        on = []
                        for comp in (0, 1):
                            OB, ZB = (0, 1) if comp == 0 else (2, 3)
                            p0 = comp * 64
                            for ik, kt in enumerate(keys):
                                bs = nbank((4, 5))
                                mm(banks[bs][:, 0:TQ], kT[:, kt * 128:(kt + 1) * 128], (qT0 if comp == 0 else qT1)[:, q0:q0 + TQ], True, True,
                                   r=[r_k[kt // 2]] + r_q[q0 // 256:(q0 + TQ) // 256], w=[bres[bs]])
                                k_ = pti % 4
                                pti += 1
                                act(pt[k_][:, 0:TQ], banks[bs][:, 0:TQ], AF.Exp, r=[bres[bs], r_sk], w=[r_pt[k_]],
                                    scale=skc[:, kt * 2 + comp: kt * 2 + comp + 1])
                                mm(banks[OB][:, 0:TQ], Vb[:, kt, :], pt[k_][:, 0:TQ], ik == 0, ik == len(keys) - 1,
                                   r=[r_v[kt // 2], r_pt[k_]], w=[bres[OB]])
                                mm(banks[ZB][:, 0:TQ], ONES1, pt[k_][:, 0:TQ], ik == 0, ik == len(keys) - 1,
                                   r=[r_cm, r_pt[k_]], w=[bres[ZB]])
                            rz, r_rz = ftmp()
                            recip(rz[:, 0:TQ], banks[ZB][:, 0:TQ], r=[bres[ZB]], w=[r_rz])
                            oc, r_oc = ftmp()
                            tt("dve", oc[:, 0:TQ], banks[OB][:, 0:TQ], rz[:, 0:TQ], ALU.mult, r=[bres[OB], r_rz], w=[r_oc])
                            on.append((oc, r_oc))
                        o32, r_o32 = on[0]
                        stt(o32[:, 0:TQ], on[1][0][:, 0:TQ], L4[:, 1:2], o32[:, 0:TQ], ALU.mult, ALU.add,
                            r=[on[1][1], r_o32, r_der], w=[r_o32])
                        osq, r_osq = btmp()
                        act(osq[:, 0:TQ], o32[:, 0:TQ], AF.Square, r=[r_o32], w=[r_osq])
                        bm = nbank((6, 7))
                        mm(banks[bm][:, 0:TQ], ONES128, osq[:, 0:TQ], True, True, r=[r_osq, r_cm], w=[bres[bm]])
                        rs, r_rs = ftmp()
                        lt, r_lt = ftmp()
                        rsqrt_act(rs[:, 0:TQ], banks[bm][:, 0:TQ], [bres[bm]], [r_rs], lt[:, 0:TQ], [r_lt])
                        tt("pool", o32[:, 0:TQ], o32[:, 0:TQ], rs[:, 0:TQ], ALU.mult, r=[r_o32, r_rs], w=[r_o32])
                        ob, r_ob = btmp()
                        act(ob[:, 0:TQ], o32[:, 0:TQ], AF.Identity, r=[r_o32, r_der], w=[r_ob], scale=L4[:, 2:3])
                        for n in range(NCH):
                            by = nbank((6, 7))
                            mm(banks[by][:, 0:TQ], woap[:, n * 128:(n + 1) * 128], ob[:, 0:TQ], True, True,
                               r=[wores, r_ob], w=[bres[by]])
                            yt_, r_yt = ftmp()
                            act(yt_[:, 0:TQ], banks[by][:, 0:TQ], AF.Identity, r=[bres[by], r_mod], w=[r_yt],
                                scale=modT[:, l, 16 + n, s:s + 1])
                            tt("pool", xres[:, n, q0:q0 + TQ], xres[:, n, q0:q0 + TQ], yt_[:, 0:TQ], ALU.add,
                               r=[r_yt] + XR(n, q0, TQ), w=XR(n, q0, TQ))

            for b in range(nb):
                load_x(b)
                for l in range(nlayers):
                    last = l == DEPTH - 1
                    if l % 2 == 0:
                        norm_mod(b, l, "m", with_ctx=not last)
                        conv_mixer(b, l, with_ctx=not last)
                    else:
                        norm_mod(b, l, "m", with_ctx=True)
                        attention(b, l, need_ctx=not last)
                    if stop_mixer and l == nlayers - 1:
                        P.barrier()
                        P.dma("sp", lambda e: e.dma_start(out=dr["dscr"][:, :], in_=scr32[:, :]))
                        continue
                    norm_mod(b, l, "f", with_ctx=not last)
                    ffn(b, l, with_ctx=not last)
                P.barrier()
                store_x(b)
            P.barrier(engines=("sp",))

        if dry_known is None:
            Pd = Planner(nc, es, dry=True)
            WSd = WStream(Pd, wsl, None)
            build(Pd, WSd)
            known = WSd.req
        else:
            known = dry_known
        P = Planner(nc, es)
        WS = WStream(P, wsl, known)
        build(P, WS)
        assert WS.i == len(known), (WS.i, len(known))
        P.emit()
        stats = {"nops": P.nops, "nsem": P.nsem, "per_eng": {e: len(P.ops[e]) for e in ENGS}}
    return nc, stats


_CACHE = {}


def kernel(**inputs):
    inp = {k: np.asarray(v) for k, v in inputs.items()}
    ncore = 8
    B = inp["x"].shape[0]
    nb = B // ncore
    wts = pack_weights(inp)
    cmat, rope = make_consts()
    in_maps = []
    for ci in range(ncore):
        sl = slice(ci * nb, (ci + 1) * nb)
        xT = np.ascontiguousarray(inp["x"][sl].reshape(nb, SEQ, NCH, 128).transpose(0, 3, 2, 1))
        cT = np.ascontiguousarray(inp["ctx"][sl].reshape(nb, CTX, NCH, 128).transpose(0, 3, 2, 1))
        m = {"xT": xT, "ctxT": cT, "cols": pack_cols(inp, inp["c"][sl]), "cmat": cmat, "rope": rope}
        m.update(wts)
        in_maps.append(m)
    nc, _ = build_program(nb=nb)
    res = run_bass_kernel_spmd(nc, in_maps, core_ids=list(range(ncore)))
    outs = []
    for ci in range(ncore):
        o = np.asarray(res.results[ci]["out"])
        outs.append(o.transpose(0, 3, 2, 1).reshape(nb, SEQ, D))
    return np.ascontiguousarray(np.concatenate(outs, axis=0)).astype(np.float32, copy=False)
```

```python
import math
from contextlib import ExitStack

import numpy as np
import concourse.bass as bass
import concourse.mybir as mybir
from concourse.bass_utils import run_bass_kernel_spmd

F32 = mybir.dt.float32
BF16 = mybir.dt.bfloat16
AF = mybir.ActivationFunctionType
ALU = mybir.AluOpType

D = 1024
NCH = 8
SEQ = 2048
CTX = 256
NTOK = SEQ + CTX
DEPTH = 4
DFF = 2816
NFC = 22
EPS = 1e-6
DA_SCALE = 0.125
NSTREAM = 5


class Res:
    __slots__ = ("w", "rd")

    def __init__(self):
        self.w = None
        self.rd = {}


ENGS = ("pe", "act", "dve", "pool", "sp")
STRICT = False
SAFE_FORMS = True
ATT_LA = 2
ATT_DEFER = True


class BufPool:
    def __init__(self, aps):
        self.aps = aps
        self.res = [Res() for _ in aps]
        self.free = list(range(len(aps)))

    def get(self):
        assert self.free, "buffer pool exhausted"
        return self.free.pop(0)

    def put(self, i):
        self.free.append(i)


class Planner:
    ROLL = 30000
    NDSEM = 8

    def __init__(self, nc, es, dry=False):
        self.nc, self.es, self.dry = nc, es, dry
        self.ops = {e: [] for e in ENGS}
        self.sem = {e: None for e in ENGS}
        self.cnt = {e: 0 for e in ENGS}
        self.seen = {e: {} for e in ENGS}
        self.last = {e: None for e in ENGS}
        self.dsem = {"sp": [], "pool": []}
        self.dlast = {"sp": [None] * self.NDSEM, "pool": [None] * self.NDSEM}
        self.di = {"sp": 0, "pool": 0}
        self.nsem = 0
        self.nops = 0

    def _newsem(self):
        self.nsem += 1
        if self.dry:
            return ("sem", self.nsem)
        return self.es.enter_context(self.nc.semaphore("s%d" % self.nsem))

    def _waits(self, eng, deps):
        out = []
        seen = self.seen[eng]
        for ev in deps:
            if ev is None:
                continue
            sem, val, _ = ev
            k = id(sem)
            if seen.get(k, 0) >= val:
                continue
            seen[k] = val
            out.append((sem, val))
        return out

    def _deps(self, eng, r, w):
        deps = []
        for res in r:
            if res.w is not None:
                if not (res.w[2] == eng and eng == "pe"):
                    deps.append(res.w)
        strict = STRICT and eng != "pe"
        for res in w:
            if res.w is not None and (res.w[2] != eng or strict):
                deps.append(res.w)
            for ev in res.rd.values():
                if ev[2] != eng or strict:
                    deps.append(ev)
        return deps

    def _commit(self, ev, r, w):
        for res in r:
            res.rd[ev[2]] = ev
        for res in w:
            res.w = ev
            res.rd = {}

    def op(self, eng, fn, r=(), w=()):
        deps = self._deps(eng, r, w)
        if self.sem[eng] is None or self.cnt[eng] >= self.ROLL:
            self.sem[eng] = self._newsem()
            self.cnt[eng] = 0
        self.cnt[eng] += 1
        ev = (self.sem[eng], self.cnt[eng], eng)
        self.ops[eng].append((fn, self._waits(eng, deps), (self.sem[eng], 1)))
        self.last[eng] = ev
        self._commit(ev, r, w)
        self.nops += 1
        return ev

    def dma(self, q, fn, r=(), w=()):
        deps = self._deps(q, r, w)
        if not self.dsem[q]:
            self.dsem[q] = [self._newsem() for _ in range(self.NDSEM)]
        i = self.di[q]
        self.di[q] += 1
        slot = i % self.NDSEM
        deps.append(self.dlast[q][slot])
        ev = (self.dsem[q][slot], 16 * (i // self.NDSEM + 1), "dma_" + q + str(slot))
        self.dlast[q][slot] = ev
        self.ops[q].append((fn, self._waits(q, deps), (ev[0], 16)))
        self._commit(ev, r, w)
        self.nops += 1
        return ev

    def all_events(self):
        evs = [self.last[e] for e in ENGS if self.last[e] is not None]
        for q in ("sp", "pool"):
            evs += [e for e in self.dlast[q] if e is not None]
        return evs

    def barrier(self, engines=ENGS):
        evs = self.all_events()
        for e in engines:
            ws = self._waits(e, [ev for ev in evs if ev[2] != e])
            if ws:
                self.ops[e].append((None, ws, None))

    def emit(self):
        nc = self.nc
        ops = self.ops

        def run(e, name):
            for fn, waits, inc in ops[name]:
                for sem, val in waits:
                    e.wait_ge(sem, val)
                if fn is not None:
                    ins = fn(e)
                    ins.then_inc(inc[0], inc[1])

        with nc.Block() as block:
            @block.tensor
            def _(e):
                run(e, "pe")

            @block.scalar
            def _(e):
                run(e, "act")

            @block.vector
            def _(e):
                run(e, "dve")

            @block.gpsimd
            def _(e):
                run(e, "pool")

            @block.sync
            def _(e):
                run(e, "sp")


def _col(v, nchunk):
    return np.ascontiguousarray(v.reshape(nchunk, 128).T)


class ColPack:
    def __init__(self):
        self.blocks, self.off, self.n = [], {}, 0

    def add(self, name, arr):
        arr = np.asarray(arr, np.float32)
        assert arr.shape[0] == 128
        self.off[name] = (self.n, arr.shape[1])
        self.blocks.append(arr)
        self.n += arr.shape[1]


def col_layout():
    off, n = {}, 0

    def add(name, wd):
        nonlocal n
        off[name] = (n, wd)
        n += wd

    add("cT", NCH * NSTREAM)
    add("ada_b", DEPTH * 48)
    for l in range(DEPTH):
        add("mixg%d" % l, 8)
        add("ffng%d" % l, 8)
        add("fdw%d" % l, 44 * 3)
        add("fdb%d" % l, 44)
    for j in range(2):
        add("pw1b%d" % j, 16)
        add("dww%d" % j, 8 * 31)
        add("dwb%d" % j, 8)
        add("lng%d" % j, 8)
        add("lnb%d" % j, 8)
        add("pw2b%d" % j, 8)
        add("qng%d" % j, 1)
        add("kng%d" % j, 1)
        add("subg%d" % j, 1)
        for nm in ("lq1", "lk1", "lq2", "lk2"):
            add("%s%d" % (nm, j), 64)
    return off, n


def pack_cols(inp, c_loc):
    cp = ColPack()
    call = np.concatenate([c_loc, inp["c_ctx"][None, :]], axis=0)
    cp.add("cT", call.reshape(NSTREAM, NCH, 128).transpose(2, 1, 0).reshape(128, NCH * NSTREAM))
    cp.add("ada_b", inp["ada_b"].reshape(DEPTH, 48, 128).transpose(2, 0, 1).reshape(128, DEPTH * 48))
    for l in range(DEPTH):
        cp.add("mixg%d" % l, _col(inp["mix_norm_g"][l], 8))
        cp.add("ffng%d" % l, _col(inp["ffn_norm_g"][l], 8))
        cp.add("fdw%d" % l, inp["ffn_dw_w"][l].reshape(3, 44, 128).transpose(2, 1, 0).reshape(128, 132))
        cp.add("fdb%d" % l, _col(inp["ffn_dw_b"][l], 44))
    for j in range(2):
        cp.add("pw1b%d" % j, _col(inp["cv_pw1_b"][j], 16))
        cp.add("dww%d" % j, inp["cv_dw_w"][j].reshape(31, 8, 128).transpose(2, 1, 0).reshape(128, 248))
        cp.add("dwb%d" % j, _col(inp["cv_dw_b"][j], 8))
        cp.add("lng%d" % j, _col(inp["cv_ln_g"][j], 8))
        cp.add("lnb%d" % j, _col(inp["cv_ln_b"][j], 8))
        cp.add("pw2b%d" % j, _col(inp["cv_pw2_b"][j], 8))
        cp.add("qng%d" % j, np.tile(inp["da_qn_g"][j], 2)[:, None])
        cp.add("kng%d" % j, np.tile(inp["da_kn_g"][j], 2)[:, None])
        cp.add("subg%d" % j, inp["da_subln_g"][j][:, None])
        for nm in ("lq1", "lk1", "lq2", "lk2"):
            cp.add("%s%d" % (nm, j), np.broadcast_to(inp["da_" + nm][j][None, :], (128, 64)))
    off, n = col_layout()
    assert off == cp.off and n == cp.n
    return np.ascontiguousarray(np.concatenate(cp.blocks, axis=1))


def _kmaj(w):
    K, N = w.shape
    return w.reshape(K // 128, 128, N).transpose(1, 0, 2)


def pack_weights(inp):
    out = {}
    a = inp["ada_w"].reshape(DEPTH, 8, 128, 12, 512).transpose(0, 3, 2, 1, 4)
    out["adap"] = np.ascontiguousarray(a).reshape(DEPTH, 12, 128, 4096)
    w = inp["cv_pw1_w"].reshape(2, 8, 128, 2, 4, 2, 128)
    w = w.transpose(0, 4, 2, 1, 3, 5, 6)
    out["pw1p"] = np.ascontiguousarray(w).reshape(2, 4, 128, 4096)
    w = inp["cv_pw2_w"].reshape(2, 8, 128, 2, 512).transpose(0, 3, 2, 1, 4)
    out["pw2p"] = np.ascontiguousarray(w).reshape(2, 2, 128, 4096)
    w = inp["da_wqkv"].reshape(2, 8, 128, 3, 8, 128)
    w = w.transpose(0, 4, 2, 1, 3, 5)
    out["wqkvp"] = np.ascontiguousarray(w).reshape(2, 8, 128, 3072)
    out["wop"] = np.ascontiguousarray(inp["da_wo"].reshape(2, 8, 128, 1024))
    w = inp["ffn_w_up"].reshape(DEPTH, 8, 128, 2, 11, 2, 128)
    w = w.transpose(0, 4, 2, 1, 3, 5, 6)
    out["upp"] = np.ascontiguousarray(w).reshape(DEPTH, 11, 128, 4096)
    w = inp["ffn_w_down"].reshape(DEPTH, 22, 128, 8, 128).transpose(0, 3, 2, 1, 4)
    out["dnp"] = np.ascontiguousarray(w).reshape(DEPTH, 8, 128, 2816)
    return out


def make_consts():
    cm = np.zeros((128, 7, 128), np.float32)
    cm[:, 0] = np.eye(128)
    cm[:, 1] = 1.0 / 1024.0
    blk = (np.arange(128)[:, None] // 64) == (np.arange(128)[None, :] // 64)
    cm[:, 2] = blk / 64.0
    cm[:, 3] = 1.0 / 128.0
    R = np.zeros((128, 128), np.float32)
    for m in range(128):
        if (m % 32) < 16:
            R[m + 16, m] = -1.0
        else:
            R[m - 16, m] = 1.0
    cm[:, 4] = R
    cm[:, 5] = 1.0
    ind = np.zeros((128, 128), np.float32)
    ind[:64, 0] = 1.0 / 64.0
    ind[64:, 1] = 1.0 / 64.0
    cm[:, 6] = ind
    t = np.arange(SEQ)
    row = (t // 64).astype(np.float32)
    colp = (t % 64).astype(np.float32)
    half = 32
    inv = (10000.0 ** (-np.arange(0, half, 2, dtype=np.float32) / half)).astype(np.float32)
    ang_r = row[:, None] * inv
    ang_c = colp[:, None] * inv
    ang = np.concatenate([ang_r, ang_r, ang_c, ang_c], axis=-1).astype(np.float32)
    cos = np.ones((128, NTOK), np.float32)
    sin = np.zeros((128, NTOK), np.float32)
    cos[:, CTX:] = np.tile(np.cos(ang).T, (2, 1))
    sin[:, CTX:] = np.tile(np.sin(ang).T, (2, 1))
    return cm.reshape(128, 7 * 128), np.ascontiguousarray(np.stack([cos, sin], axis=1)).reshape(128, 2 * NTOK)


class WStream:
    NSLOT = 3
    AHEAD = 2

    def __init__(self, P, slots_ap, known=None):
        self.P = P
        self.slots_ap = slots_ap
        self.known = known
        self.req = []
        self.res = [Res() for _ in range(self.NSLOT)]
        self.issued = 0
        self.i = 0

    def _issue(self, idx):
        src, nel = self.known[idx]
        slot = idx % self.NSLOT
        dst = self.slots_ap[:, slot * 4096: slot * 4096 + nel]
        self.P.dma("pool", lambda e, d=dst, s=src: e.dma_start(out=d, in_=s, max_dma_last_dim=4096), w=[self.res[slot]])

    def next(self, src, nel):
        idx = self.i
        self.i += 1
        self.req.append((src, nel))
        if self.known is None:
            slot = idx % self.NSLOT
            return self.slots_ap[:, slot * 4096: slot * 4096 + nel], self.res[slot]
        while self.issued < min(len(self.known), idx + 1 + self.AHEAD):
            self._issue(self.issued)
            self.issued += 1
        slot = idx % self.NSLOT
        return self.slots_ap[:, slot * 4096: slot * 4096 + nel], self.res[slot]


def build_program(nb=4, nlayers=DEPTH, debug=False, dry_known=None, stop_mixer=False):
    nc = bass.Bass("TRN2", target_bir_lowering=False)
    coff, ncol = col_layout()

    dr = {}
    dr["xT"] = nc.dram_tensor("xT", [nb, 128, NCH, SEQ], F32, kind="ExternalInput").ap()
    dr["ctxT"] = nc.dram_tensor("ctxT", [nb, 128, NCH, CTX], F32, kind="ExternalInput").ap()
    dr["cols"] = nc.dram_tensor("cols", [128, ncol], F32, kind="ExternalInput").ap()
    dr["cmat"] = nc.dram_tensor("cmat", [128, 7 * 128], F32, kind="ExternalInput").ap()
    dr["rope"] = nc.dram_tensor("rope", [128, 2 * NTOK], F32, kind="ExternalInput").ap()
    dr["adap"] = nc.dram_tensor("adap", [DEPTH, 12, 128, 4096], F32, kind="ExternalInput").ap()
    dr["pw1p"] = nc.dram_tensor("pw1p", [2, 4, 128, 4096], F32, kind="ExternalInput").ap()
    dr["pw2p"] = nc.dram_tensor("pw2p", [2, 2, 128, 4096], F32, kind="ExternalInput").ap()
    dr["wqkvp"] = nc.dram_tensor("wqkvp", [2, 8, 128, 3072], F32, kind="ExternalInput").ap()
    dr["wop"] = nc.dram_tensor("wop", [2, 8, 128, 1024], F32, kind="ExternalInput").ap()
    dr["upp"] = nc.dram_tensor("upp", [DEPTH, 11, 128, 4096], F32, kind="ExternalInput").ap()
    dr["dnp"] = nc.dram_tensor("dnp", [DEPTH, 8, 128, 2816], F32, kind="ExternalInput").ap()
    dr["out"] = nc.dram_tensor("out", [nb, 128, NCH, SEQ], F32, kind="ExternalOutput").ap()
    if debug:
        dr["octx"] = nc.dram_tensor("octx", [nb, 128, NCH, CTX], F32, kind="ExternalOutput").ap()
        dr["dscr"] = nc.dram_tensor("dscr", [128, 14336], F32, kind="ExternalOutput").ap()

    with ExitStack() as es:
        def sb(name, shape, dt):
            return es.enter_context(nc.sbuf_tensor(name, shape, dt))

        xres_t = sb("xres", [128, NCH * NTOK], F32)
        hbuf_t = sb("hbuf", [128, NCH * NTOK], BF16)
        wsl_t = sb("wsl", [128, 3 * 4096], BF16)
        cols_t = sb("colsb", [128, ncol], F32)
        cm_t = sb("cmb", [128, 7 * 128], BF16)
        modT_t = sb("modT", [128, DEPTH * 48 * NSTREAM], F32)
        der_t = sb("der", [128, 1280], F32)
        SCR_W = 14336
        scr_t = sb("scr", [128, SCR_W], F32)
        banks = [es.enter_context(nc.psum_tensor("bank%d" % i, [128, 512], F32)) for i in range(8)]

        xres = xres_t[:, :].rearrange("p (c t) -> p c t", c=NCH)
        hbuf = hbuf_t[:, :].rearrange("p (c t) -> p c t", c=NCH)
        wsl = wsl_t[:, :]
        cols = cols_t[:, :]
        cm = cm_t[:, :]
        scr32 = scr_t[:, :]
        scr16 = scr_t[:, :].bitcast(BF16)
        IDENT, ONES1024, BLK64, ONES128, RROT, ONES1, IND2 = [cm[:, i * 128:(i + 1) * 128] for i in range(7)]

        def C(name, j=0, n=1):
            o, wd = coff[name]
            assert j + n <= wd
            return cols[:, o + j: o + j + n]

        modT = modT_t[:, :].rearrange("p (l j s) -> p l j s", l=DEPTH, j=48)
        der = der_t[:, :]
        DA_M = der[:, 0:160].rearrange("p (l c s) -> p l c s", l=DEPTH, c=8)
        DA_F = der[:, 160:320].rearrange("p (l c s) -> p l c s", l=DEPTH, c=8)
        DGB = der[:, 320:480].rearrange("p (l c s) -> p l c s", l=DEPTH, c=8)
        HB1 = der[:, 480:512].rearrange("p (j c) -> p j c", j=2)
        LAM = der[:, 512:520]
        TMP = der[:, 520:1024]
        SCT = der[:, 1024:1064]

        def build(P, WS):
            xr = [[Res() for _ in range(9)] for _ in range(NCH)]
            hr = [[Res() for _ in range(9)] for _ in range(NCH)]
            bres = [Res() for _ in range(8)]
            r_cols, r_cm, r_mod, r_der = Res(), Res(), Res(), Res()
            bank_i = [0]

            def XR(c, t0, T):
                return [xr[c][k] for k in range(t0 // 256, (t0 + T + 255) // 256)]

            def HR(c, t0, T):
                return [hr[c][k] for k in range(t0 // 256, (t0 + T + 255) // 256)]

            def HRall(t0, T):
                return [r for c in range(NCH) for r in HR(c, t0, T)]

            def nbank(pool=(0, 1, 2, 3, 4, 5, 6, 7)):
                b = pool[bank_i[0] % len(pool)]
                bank_i[0] += 1
                return b

            def mm(out, lhsT, rhs, start, stop, r, w):
                return P.op("pe", lambda e: e.matmul(out, lhsT=lhsT, rhs=rhs, start=start, stop=stop), r=r, w=w)

            def act(out, in_, func, r, w, bias=0.0, scale=1.0):
                return P.op("act", lambda e: e.activation(out=out, in_=in_, func=func, bias=bias, scale=scale), r=r, w=w)

            def tt(eng, out, in0, in1, op, r, w):
                return P.op(eng, lambda e: e.tensor_tensor(out=out, in0=in0, in1=in1, op=op), r=r, w=w)

            def ts(eng, out, in0, s1, s2, op0, op1, r, w):
                return P.op(eng, lambda e: e.tensor_scalar(out=out, in0=in0, scalar1=s1, scalar2=s2, op0=op0, op1=op1), r=r, w=w)

            def stt(out, in0, scalar, in1, op0, op1, r, w):
                return P.op("dve", lambda e: e.scalar_tensor_tensor(out=out, in0=in0, scalar=scalar, in1=in1, op0=op0, op1=op1), r=r, w=w)

            def recip(out, in_, r, w):
                return P.op("dve", lambda e: e.reciprocal(out=out, in_=in_), r=r, w=w)

            def rsqrt_act(out, in_, r_in, r_out, tmp, r_tmp, post_scale=1.0):
                act(tmp, in_, AF.Ln, r=r_in, w=r_tmp, bias=EPSB)
                b = 0.0 if post_scale == 1.0 else None
                if b is None:
                    act(out, tmp, AF.Exp, r=r_tmp, w=r_out, scale=-0.5, bias=LNSC)
                else:
                    act(out, tmp, AF.Exp, r=r_tmp, w=r_out, scale=-0.5)

            P.dma("sp", lambda e: e.dma_start(out=cols, in_=dr["cols"][:, :]), w=[r_cols])
            P.dma("pool", lambda e: e.dma_start(out=cm, in_=dr["cmat"][:, :], max_dma_last_dim=3584), w=[r_cm])
            CONSTC = der[:, 1064:1072]
            P.op("dve", lambda e: e.memset(CONSTC[:, 0:1], EPS), w=[r_der])
            P.op("dve", lambda e: e.memset(CONSTC[:, 1:2], math.log(DA_SCALE)), w=[r_der])
            P.op("dve", lambda e: e.memset(CONSTC[:, 2:3], 1.0), w=[r_der])
            P.op("dve", lambda e: e.memset(CONSTC[64:128, 2:3], 0.0), w=[r_der])
            P.op("dve", lambda e: e.memset(CONSTC[:, 3:4], 0.0), w=[r_der])
            P.op("dve", lambda e: e.memset(CONSTC[64:128, 3:4], 1.0), w=[r_der])
            MASK0, MASK1 = CONSTC[:, 2:3], CONSTC[:, 3:4]
            EPSB_ = CONSTC[:, 0:1]
            LNSC_ = CONSTC[:, 1:2]

            act(SCT, C("cT", 0, 40), AF.Silu, r=[r_cols], w=[r_der])
            aslot = [scr32[:, 0:4096], scr32[:, 4096:8192]]
            ares = [Res(), Res()]
            pi = 0
            for l in range(nlayers):
                mb = nbank()
                for j12 in range(12):
                    s = pi % 2
                    pi += 1
                    src = dr["adap"][l, j12]
                    P.dma("sp", lambda e, d=aslot[s], s_=src: e.dma_start(out=d, in_=s_), w=[ares[s]])
                    wv = aslot[s].rearrange("p (k n) -> p k n", k=8)
                    for jj in range(4):
                        j = j12 * 4 + jj
                        for kc in range(8):
                            mm(banks[mb][:, j * 8: j * 8 + NSTREAM], wv[:, kc, jj * 128:(jj + 1) * 128],
                               SCT[:, kc * NSTREAM:(kc + 1) * NSTREAM], kc == 0, kc == 7,
                               r=[ares[s], r_der], w=[bres[mb]])
                o, _ = coff["ada_b"]
                bias = cols[:, o + l * 48: o + (l + 1) * 48].unsqueeze(2).broadcast_to([128, 48, NSTREAM])
                src = banks[mb][:, 0:384].rearrange("p (j e) -> p j e", e=8)[:, :, 0:NSTREAM]
                tt("dve", modT[:, l], src, bias, ALU.add, r=[bres[mb], r_cols], w=[r_mod])
                for which, dst, gname in ((1, DA_M, "mixg%d" % l), (4, DA_F, "ffng%d" % l)):
                    g = C(gname, 0, 8).unsqueeze(2).broadcast_to([128, 8, NSTREAM])
                    P.op("dve", lambda e, d=dst[:, l], s_=modT[:, l, which * 8:(which + 1) * 8, :], g=g:
                         e.scalar_tensor_tensor(out=d, in0=s_, scalar=1.0, in1=g, op0=ALU.add, op1=ALU.mult),
                         r=[r_mod, r_cols], w=[r_der])
                if l % 2 == 0:
                    jv = l // 2
                    b2 = C("pw2b%d" % jv, 0, 8).unsqueeze(2).broadcast_to([128, 8, NSTREAM])
                    tt("dve", DGB[:, l], modT[:, l, 16:24, :], b2, ALU.mult, r=[r_mod, r_cols], w=[r_der])
                    ts("dve", HB1[:, jv], C("pw1b%d" % jv, 0, 16), 0.5, None, ALU.mult, ALU.bypass, r=[r_cols], w=[r_der])
                else:
                    jv = l // 2
                    lam_init = 0.8 - 0.6 * math.exp(-0.3 * l)
                    L4 = LAM[:, jv * 4: jv * 4 + 4]
                    for k, (a, b) in enumerate((("lq1", "lk1"), ("lq2", "lk2"))):
                        tt("dve", TMP[:, 0:64], C("%s%d" % (a, jv), 0, 64), C("%s%d" % (b, jv), 0, 64), ALU.mult,
                           r=[r_cols, r_der], w=[r_der])
                        P.op("dve", lambda e, o_=TMP[:, 64 + k: 65 + k]: e.tensor_reduce(out=o_, in_=TMP[:, 0:64],
                             axis=mybir.AxisListType.X, op=ALU.add), r=[r_der], w=[r_der])
                    act(TMP[:, 66:68], TMP[:, 64:66], AF.Exp, r=[r_der], w=[r_der])
                    tt("dve", L4[:, 0:1], TMP[:, 66:67], TMP[:, 67:68], ALU.subtract, r=[r_der], w=[r_der])
                    ts("dve", L4[:, 0:1], L4[:, 0:1], lam_init, None, ALU.add, ALU.bypass, r=[r_der], w=[r_der])
                    ts("dve", L4[:, 1:2], L4[:, 0:1], -1.0, None, ALU.mult, ALU.bypass, r=[r_der], w=[r_der])
                    ts("dve", L4[:, 2:3], C("subg%d" % jv), 1.0 - lam_init, None, ALU.mult, ALU.bypass, r=[r_cols, r_der], w=[r_der])
            P.barrier()

            EPSB, LNSC = EPSB_, LNSC_

            def load_x(b):
                for c in range(NCH):
                    P.dma("sp", lambda e, c=c: e.dma_start(out=xres[:, c, CTX:NTOK], in_=dr["xT"][b, :, c, :]), w=XR(c, CTX, SEQ))
                for c in range(NCH):
                    P.dma("sp", lambda e, c=c: e.dma_start(out=xres[:, c, 0:CTX], in_=dr["ctxT"][b, :, c, :]), w=XR(c, 0, CTX))

            def store_x(b):
                for c in range(NCH):
                    P.dma("sp", lambda e, c=c: e.dma_start(out=dr["out"][b, :, c, :], in_=xres[:, c, CTX:NTOK]), r=XR(c, CTX, SEQ))
                if debug:
                    for c in range(NCH):
                        P.dma("sp", lambda e, c=c: e.dma_start(out=dr["octx"][b, :, c, :], in_=xres[:, c, 0:CTX]), r=XR(c, 0, CTX))

            def tiles_for(with_ctx):
                t = [(0, CTX)] if with_ctx else []
                return t + [(CTX + 512 * i, 512) for i in range(4)]

            def stream_of(b, t0):
                return 4 if t0 < CTX else b

            def norm_mod(b, l, which, with_ctx):
                P.barrier()
                A = DA_M if which == "m" else DA_F
                shj = 0 if which == "m" else 24
                sq = [scr16[:, i * 4096:(i + 1) * 4096].rearrange("p (c t) -> p c t", c=8) for i in range(2)]
                r_sq = [Res(), Res()]
                rstd = scr32[:, 4096:4096 + NTOK]
                r_rstd = [Res() for _ in range(9)]
                lnt = [scr32[:, 6400 + i * 512: 6400 + (i + 1) * 512] for i in range(2)]
                r_lnt = [Res(), Res()]
                tmp = [scr32[:, 7424 + i * 512: 7424 + (i + 1) * 512] for i in range(4)]
                r_tmp = [Res() for _ in range(4)]
                ti = 0
                for ii, (t0, T) in enumerate(tiles_for(with_ctx)):
                    s = stream_of(b, t0)
                    q = ii % 2
                    xin = [r for c in range(NCH) for r in XR(c, t0, T)]
                    act(sq[q][:, :, 0:T], xres[:, :, t0:t0 + T], AF.Square, r=xin, w=[r_sq[q]])
                    bk = nbank()
                    for c in range(NCH):
                        mm(banks[bk][:, 0:T], ONES1024, sq[q][:, c, 0:T], c == 0, c == NCH - 1, r=[r_sq[q], r_cm], w=[bres[bk]])
                    rr = r_rstd[t0 // 256:(t0 + T) // 256]
                    rsqrt_act(rstd[:, t0:t0 + T], banks[bk][:, 0:T], [bres[bk]], rr, lnt[q][:, 0:T], [r_lnt[q]])
                    for c in range(NCH):
                        k = ti % 4
                        ti += 1
                        eng = "pool" if (c % 2 == 0) else "dve"
                        tt(eng, tmp[k][:, 0:T], xres[:, c, t0:t0 + T], rstd[:, t0:t0 + T], ALU.mult,
                           r=XR(c, t0, T) + rr, w=[r_tmp[k]])
                        act(hbuf[:, c, t0:t0 + T], tmp[k][:, 0:T], AF.Identity, r=[r_tmp[k], r_der, r_mod], w=HR(c, t0, T),
                            scale=A[:, l, c, s:s + 1], bias=modT[:, l, shj + c, s:s + 1])

            def conv_mixer(b, l, with_ctx):
                P.barrier()
                jv = l // 2
                diag = [scr16[:, i * 3968:(i + 1) * 3968].rearrange("p (k m) -> p k m", k=31) for i in range(2)]
                r_diag = [Res(), Res()]
                r_diag_d = [Res(), Res()]
                up_l = [scr16[:, 7936 + i * 752: 7936 + (i + 1) * 752] for i in range(3)]
                up_c = [scr16[:, 10192 + i * 288: 10192 + (i + 1) * 288] for i in range(3)]
                r_up = [Res() for _ in range(3)]
                vbuf = scr32[:, 5528:5528 + 4096].rearrange("p (c t) -> p c t", c=8)
                r_vb = [Res() for _ in range(8)]
                zb = scr16[:, 19248:19248 + 4096].rearrange("p (c t) -> p c t", c=8)
                r_zb = [Res() for _ in range(8)]
                st16 = [scr16[:, 23344 + i * 512: 23344 + (i + 1) * 512] for i in range(4)]
                r_st = [Res() for _ in range(4)]
                f32t = [scr32[:, 12696 + i * 512: 12696 + (i + 1) * 512] for i in range(3)]
                r_ft = [Res() for _ in range(3)]
                for i in range(3):
                    P.op("pool", lambda e, a=up_l[i]: e.memset(a, 0.0), w=[r_up[i]])
                zc = [False]
                ui = 0
                for (t0, T) in tiles_for(with_ctx):
                    s = stream_of(b, t0)
                    isc = t0 < CTX
                    if isc:
                        for i in range(3):
                            P.op("pool", lambda e, a=up_c[i]: e.memset(a, 0.0), w=[r_up[i]])
                    R_, W_ = (1, 256) if isc else (8, 64)
                    hin = HRall(t0, T)
                    MB, QB = 6, 7
                    for jp in range(4):
                        wap, wres = WS.next(dr["pw1p"][jv, jp], 4096)
                        wv = wap.rearrange("p (k n) -> p k n", k=8)
                        for cc in range(2):
                            c = 2 * jp + cc
                            ba, bg = nbank((0, 1, 2, 3, 4, 5)), nbank((0, 1, 2, 3, 4, 5))
                            for kc in range(NCH):
                                mm(banks[ba][:, 0:T], wv[:, kc, cc * 128:(cc + 1) * 128], hbuf[:, kc, t0:t0 + T], kc == 0, kc == 7,
                                   r=[wres] + HR(kc, t0, T), w=[bres[ba]])
                            for kc in range(NCH):
                                mm(banks[bg][:, 0:T], wv[:, kc, (2 + cc) * 128:(3 + cc) * 128], hbuf[:, kc, t0:t0 + T], kc == 0, kc == 7,
                                   r=[wres] + HR(kc, t0, T), w=[bres[bg]])
                            fa, fb = 0, 1
                            act(f32t[fa][:, 0:T], banks[bg][:, 0:T], AF.Tanh, r=[bres[bg], r_der], w=[r_ft[fa]],
                                scale=0.5, bias=HB1[:, jv, 8 + c: 9 + c])
                            act(f32t[fb][:, 0:T], banks[ba][:, 0:T], AF.Identity, r=[bres[ba], r_der], w=[r_ft[fb]],
                                scale=0.5, bias=HB1[:, jv, c: c + 1])
                            u = ui % 3
                            ui += 1
                            if isc:
                                upv = up_c[u][:, 0:286].rearrange("p (r w) -> p r w", r=1)
                            else:
                                upv = up_l[u][:, 0:752].rearrange("p (r w) -> p r w", r=8)
                            stt(upv[:, :, 15:15 + W_], f32t[fa][:, 0:T].rearrange("p (r w) -> p r w", r=R_), 1.0,
                                f32t[fb][:, 0:T].rearrange("p (r w) -> p r w", r=R_), ALU.add, ALU.mult,
                                r=[r_ft[fa], r_ft[fb]], w=[r_up[u]])
                            dg = c % 2
                            for k in range(31):
                                on_dve = (k % 3 == 0)
                                P.op("dve" if on_dve else "pool", lambda e, o_=diag[dg][:, k, :], s_=C("dww%d" % jv, c * 31 + k):
                                     e.tensor_scalar(out=o_, in0=IDENT, scalar1=s_, scalar2=0.0, op0=ALU.mult, op1=ALU.add),
                                     r=[r_cols, r_cm], w=[(r_diag_d if on_dve else r_diag)[dg]])
                            bv = nbank((0, 1, 2, 3, 4, 5))
                            for k in range(31):
                                mm(banks[bv][:, 0:T].rearrange("p (r w) -> p r w", r=R_), diag[dg][:, k, :], upv[:, :, k:k + W_],
                                   k == 0, k == 30, r=[r_diag[dg], r_diag_d[dg], r_up[u]], w=[bres[bv]])
                            act(vbuf[:, c, 0:T], banks[bv][:, 0:T], AF.Identity, r=[bres[bv], r_cols], w=[r_vb[c]],
                                bias=C("dwb%d" % jv, c))
                            s0, s1 = (2 * c) % 4, (2 * c + 1) % 4
                            act(st16[s0][:, 0:T], banks[bv][:, 0:T], AF.Identity, r=[bres[bv], r_cols], w=[r_st[s0]],
                                bias=C("dwb%d" % jv, c))
                            act(st16[s1][:, 0:T], banks[bv][:, 0:T], AF.Square, r=[bres[bv], r_cols], w=[r_st[s1]],
                                bias=C("dwb%d" % jv, c))
                            mm(banks[MB][:, 0:T], ONES1024, st16[s0][:, 0:T], c == 0, c == 7, r=[r_st[s0], r_cm], w=[bres[MB]])
                            mm(banks[QB][:, 0:T], ONES1024, st16[s1][:, 0:T], c == 0, c == 7, r=[r_st[s1], r_cm], w=[bres[QB]])
                    mu, var, rs = f32t[0], f32t[1], f32t[2]
                    P.op("dve", lambda e, o_=mu[:, 0:T], i_=banks[MB][:, 0:T]: e.tensor_copy(out=o_, in_=i_), r=[bres[MB]], w=[r_ft[0]])
                    tt("dve", var[:, 0:T], mu[:, 0:T], mu[:, 0:T], ALU.mult, r=[r_ft[0]], w=[r_ft[1]])
                    tt("dve", var[:, 0:T], banks[QB][:, 0:T], var[:, 0:T], ALU.subtract, r=[bres[QB], r_ft[1]], w=[r_ft[1]])
                    rsqrt_act(rs[:, 0:T], var[:, 0:T], [r_ft[1]], [r_ft[2]], var[:, 0:T], [r_ft[1]])
                    for c in range(NCH):
                        k0 = c % 4
                        dtmp = st16
                        tt("dve", vbuf[:, c, 0:T], vbuf[:, c, 0:T], mu[:, 0:T], ALU.subtract, r=[r_vb[c], r_ft[0]], w=[r_vb[c]])
                        tt("pool", vbuf[:, c, 0:T], vbuf[:, c, 0:T], rs[:, 0:T], ALU.mult, r=[r_vb[c], r_ft[2]], w=[r_vb[c]])
                        act(zb[:, c, 0:T], vbuf[:, c, 0:T], AF.Silu, r=[r_vb[c], r_cols], w=[r_zb[c]],
                            scale=C("lng%d" % jv, c), bias=C("lnb%d" % jv, c))
                    yt = [scr16[:, 23344 + i * 1024: 23344 + (i + 1) * 1024].bitcast(F32) for i in range(2)]
                    for jp in range(2):
                        wap, wres = WS.next(dr["pw2p"][jv, jp], 4096)
                        wv = wap.rearrange("p (k n) -> p k n", k=8)
                        for cc in range(4):
                            n = 4 * jp + cc
                            by = nbank((0, 1, 2, 3, 4, 5))
                            for kc in range(NCH):
                                mm(banks[by][:, 0:T], wv[:, kc, cc * 128:(cc + 1) * 128], zb[:, kc, 0:T], kc == 0, kc == 7,
                                   r=[wres, r_zb[kc]], w=[bres[by]])
                            y = n % 2
                            yr = [r_st[2 * y], r_st[2 * y + 1]]
                            act(yt[y][:, 0:T], banks[by][:, 0:T], AF.Identity, r=[bres[by], r_mod, r_der], w=yr,
                                scale=modT[:, l, 16 + n, s:s + 1], bias=DGB[:, l, n, s:s + 1])
                            tt("pool", xres[:, n, t0:t0 + T], xres[:, n, t0:t0 + T], yt[y][:, 0:T], ALU.add,
                               r=yr + XR(n, t0, T), w=XR(n, t0, T))

            def ffn(b, l, with_ctx):
                P.barrier()
                if with_ctx:
                    groups = [[(0, 256), (256, 512)], [(768, 512), (1280, 256)], [(1536, 512), (2048, 256)]]
                else:
                    groups = [[(256, 512), (768, 256)], [(1024, 512), (1536, 256)], [(1792, 512)]]
                gb = scr16[:, 0:NFC * 768].rearrange("p (c t) -> p c t", c=NFC)
                r_g = [[Res(), Res()] for _ in range(NFC)]
                ca = [scr32[:, 8448 + i * 512: 8448 + (i + 1) * 512] for i in range(4)]
                cv = [scr32[:, 10496 + i * 512: 10496 + (i + 1) * 512] for i in range(4)]
                yt = [scr32[:, 12544 + i * 512: 12544 + (i + 1) * 512] for i in range(3)]
                r_ca = [Res() for _ in range(4)]
                r_cv = [Res() for _ in range(4)]
                r_yt = [Res() for _ in range(3)]
                ki = 0
                yi = 0
                for grp in groups:
                    for jp in range(11):
                        wap, wres = WS.next(dr["upp"][l, jp], 4096)
                        wv = wap.rearrange("p (k n) -> p k n", k=8)
                        for cc in range(2):
                            ch = 2 * jp + cc
                            g0 = 0
                            for gi, (t0, T) in enumerate(grp):
                                isc = t0 < CTX
                                R_, W_ = (1, 256) if isc else (T // 64, 64)
                                ba, bv = nbank(), nbank()
                                for kc in range(NCH):
                                    mm(banks[ba][:, 0:T], wv[:, kc, cc * 128:(cc + 1) * 128], hbuf[:, kc, t0:t0 + T], kc == 0, kc == 7,
                                       r=[wres] + HR(kc, t0, T), w=[bres[ba]])
                                for kc in range(NCH):
                                    mm(banks[bv][:, 0:T], wv[:, kc, (2 + cc) * 128:(3 + cc) * 128], hbuf[:, kc, t0:t0 + T], kc == 0, kc == 7,
                                       r=[wres] + HR(kc, t0, T), w=[bres[bv]])
                                k = ki % 4
                                ki += 1
                                for (bk, dst, rdst, chn) in ((ba, ca[k], r_ca[k], ch), (bv, cv[k], r_cv[k], NFC + ch)):
                                    act(dst[:, 0:T], banks[bk][:, 0:T], AF.Identity, r=[bres[bk], r_cols], w=[rdst],
                                        scale=C("fdw%d" % l, chn * 3 + 1), bias=C("fdb%d" % l, chn))
                                    d3 = dst[:, 0:T].rearrange("p (r w) -> p r w", r=R_)
                                    p3 = banks[bk][:, 0:T].rearrange("p (r w) -> p r w", r=R_)
                                    stt(d3[:, :, 1:W_], p3[:, :, 0:W_ - 1], C("fdw%d" % l, chn * 3 + 0), d3[:, :, 1:W_],
                                        ALU.mult, ALU.add, r=[bres[bk], r_cols, rdst], w=[rdst])
                                    stt(d3[:, :, 0:W_ - 1], p3[:, :, 1:W_], C("fdw%d" % l, chn * 3 + 2), d3[:, :, 0:W_ - 1],
                                        ALU.mult, ALU.add, r=[bres[bk], r_cols, rdst], w=[rdst])
                                act(ca[k][:, 0:T], ca[k][:, 0:T], AF.Silu, r=[r_ca[k]], w=[r_ca[k]])
                                tt("pool", gb[:, ch, g0:g0 + T], ca[k][:, 0:T], cv[k][:, 0:T], ALU.mult,
                                   r=[r_ca[k], r_cv[k]], w=[r_g[ch][gi]])
                                g0 += T
                    for n in range(NCH):
                        wap, wres = WS.next(dr["dnp"][l, n], 2816)
                        wv = wap.rearrange("p (k m) -> p k m", k=NFC)
                        g0 = 0
                        for gi, (t0, T) in enumerate(grp):
                            s = stream_of(b, t0)
                            by = nbank()
                            for kc in range(NFC):
                                mm(banks[by][:, 0:T], wv[:, kc, :], gb[:, kc, g0:g0 + T], kc == 0, kc == NFC - 1,
                                   r=[wres, r_g[kc][gi]], w=[bres[by]])
                            y = yi % 3
                            yi += 1
                            act(yt[y][:, 0:T], banks[by][:, 0:T], AF.Identity, r=[bres[by], r_mod], w=[r_yt[y]],
                                scale=modT[:, l, 40 + n, s:s + 1])
                            tt("pool", xres[:, n, t0:t0 + T], xres[:, n, t0:t0 + T], yt[y][:, 0:T], ALU.add,
                               r=[r_yt[y]] + XR(n, t0, T), w=XR(n, t0, T))
                            g0 += T

            def attention(b, l, need_ctx):
                P.barrier()
                jv = l // 2
                L4 = LAM[:, jv * 4: jv * 4 + 4]
                cosT = scr32[:, 0:NTOK]
                sinT = scr32[:, NTOK:2 * NTOK]
                r_rope = Res()
                P.dma("sp", lambda e: e.dma_start(out=scr32[:, 0:2 * NTOK], in_=dr["rope"][:, :]), w=[r_rope])
                qT = scr16[:, 9216:9216 + NTOK]
                kT0 = scr16[:, 11520:11520 + NTOK]
                kT1 = scr16[:, 13824:13824 + NTOK]
                Vb = scr16[:, 16128:16128 + 18 * 128].rearrange("p (k e) -> p k e", k=18)
                r_q = [Res() for _ in range(9)]
                r_k = [Res() for _ in range(9)]
                r_v = [Res() for _ in range(9)]
                PT = BufPool([scr16[:, 18432 + i * 512: 18432 + (i + 1) * 512] for i in range(4)])
                FT = BufPool([scr32[:, 10240 + i * 512: 10240 + (i + 1) * 512] for i in range(6)])
                BT = BufPool([scr16[:, 26624 + i * 512: 26624 + (i + 1) * 512] for i in range(4)])
                skc = TMP[:, 100:136]
                sktmp = TMP[:, 140:176]
                r_sk = Res()
                PB = (0, 1, 2, 3, 4, 5, 6)
                skb = 7
                tl_all = tiles_for(True)
                for hd in range(8):
                    wap, wres = WS.next(dr["wqkvp"][jv, hd], 3072)
                    wv = wap.rearrange("p (k n) -> p k n", k=8)
                    items = []
                    for which in (1, 0):
                        for (t0, T) in tl_all:
                            if which == 0 and t0 < CTX and not need_ctx:
                                continue
                            items.append((which, t0, T))

                    def stage_a(which, t0, T):
                        gcol = C("qng%d" % jv) if which == 0 else C("kng%d" % jv)
                        bp = nbank(PB)
                        for kc in range(NCH):
                            mm(banks[bp][:, 0:T], wv[:, kc, which * 128:(which + 1) * 128], hbuf[:, kc, t0:t0 + T], kc == 0, kc == 7,
                               r=[wres] + HR(kc, t0, T), w=[bres[bp]])
                        ig, isq, it1 = BT.get(), BT.get(), FT.get()
                        act(BT.aps[ig][:, 0:T], banks[bp][:, 0:T], AF.Identity, r=[bres[bp], r_cols], w=[BT.res[ig]], scale=gcol)
                        act(BT.aps[isq][:, 0:T], banks[bp][:, 0:T], AF.Square, r=[bres[bp]], w=[BT.res[isq]])
                        if SAFE_FORMS:
                            act(FT.aps[it1][:, 0:T], banks[bp][:, 0:T], AF.Identity, r=[bres[bp], r_cols], w=[FT.res[it1]], scale=gcol)
                            tt("dve", FT.aps[it1][:, 0:T], FT.aps[it1][:, 0:T], cosT[:, t0:t0 + T], ALU.mult, r=[FT.res[it1], r_rope], w=[FT.res[it1]])
                        else:
                            stt(FT.aps[it1][:, 0:T], banks[bp][:, 0:T], gcol, cosT[:, t0:t0 + T], ALU.mult, ALU.mult,
                                r=[bres[bp], r_cols, r_rope], w=[FT.res[it1]])
                        return (which, t0, T, ig, isq, it1)

                    def stage_b(ctx):
                        which, t0, T, ig, isq, it1 = ctx
                        br = nbank(PB)
                        mm(banks[br][:, 0:T], RROT, BT.aps[ig][:, 0:T], True, True, r=[BT.res[ig], r_cm], w=[bres[br]])
                        it2 = FT.get()
                        t1, t2 = FT.aps[it1], FT.aps[it2]
                        tt("dve", t2[:, 0:T], banks[br][:, 0:T], sinT[:, t0:t0 + T], ALU.mult, r=[bres[br], r_rope], w=[FT.res[it2]])
                        if which == 0:
                            rr = r_q[t0 // 256:(t0 + T) // 256]
                            bs = nbank(PB)
                            mm(banks[bs][:, 0:T], BLK64, BT.aps[isq][:, 0:T], True, True, r=[BT.res[isq], r_cm], w=[bres[bs]])
                            irq, ilt = FT.get(), FT.get()
                            rsqrt_act(FT.aps[irq][:, 0:T], banks[bs][:, 0:T], [bres[bs]], [FT.res[irq]], FT.aps[ilt][:, 0:T], [FT.res[ilt]])
                            tt("pool", t1[:, 0:T], t1[:, 0:T], t2[:, 0:T], ALU.add, r=[FT.res[it1], FT.res[it2]], w=[FT.res[it1]])
                            tt("dve", qT[:, t0:t0 + T], t1[:, 0:T], FT.aps[irq][:, 0:T], ALU.mult, r=[FT.res[it1], FT.res[irq]], w=rr)
                            FT.put(irq)
                            FT.put(ilt)
                        else:
                            rr = r_k[t0 // 256:(t0 + T) // 256]
                            tt("pool", t1[:, 0:T], t1[:, 0:T], t2[:, 0:T], ALU.add, r=[FT.res[it1], FT.res[it2]], w=[FT.res[it1]])
                            ts("dve", kT0[:, t0:t0 + T], t1[:, 0:T], MASK0, None, ALU.mult, ALU.bypass, r=[FT.res[it1], r_der], w=rr)
                            ts("dve", kT1[:, t0:t0 + T], t1[:, 0:T], MASK1, None, ALU.mult, ALU.bypass, r=[FT.res[it1], r_der], w=rr)
                            for kk in range(T // 128):
                                kt = t0 // 128 + kk
                                mm(banks[skb][:, kt * 2: kt * 2 + 2], BT.aps[isq][:, kk * 128:(kk + 1) * 128], IND2[:, 0:2], True, True,
                                   r=[BT.res[isq], r_cm], w=[bres[skb]])
                        FT.put(it1)
                        FT.put(it2)
                        BT.put(ig)
                        BT.put(isq)

                    ctxs = []
                    for i, it in enumerate(items):
                        ctxs.append(stage_a(*it))
                        if i >= 1:
                            stage_b(ctxs[i - 1])
                    stage_b(ctxs[-1])
                    rsqrt_act(skc, banks[skb][:, 0:36], [bres[skb]], [r_sk], sktmp, [r_sk], post_scale=DA_SCALE)
                    for g4 in range(5):
                        kts = list(range(g4 * 4, min(18, g4 * 4 + 4)))
                        bv = nbank(PB)
                        for i, kt in enumerate(kts):
                            for kc in range(NCH):
                                mm(banks[bv][:, i * 128:(i + 1) * 128], hbuf[:, kc, kt * 128:(kt + 1) * 128], wv[:, kc, 256:384], kc == 0, kc == 7,
                                   r=[wres] + HR(kc, kt * 128, 128), w=[bres[bv]])
                        n_ = len(kts)
                        act(Vb[:, kts[0]:kts[0] + n_, :], banks[bv][:, 0:n_ * 128].rearrange("p (k e) -> p k e", k=n_), AF.Identity,
                            r=[bres[bv]], w=r_v[kts[0] // 2:(kts[-1]) // 2 + 1])
                    woap, wores = WS.next(dr["wop"][jv, hd], 1024)
                    qsets = [(CTX + 512 * i, 512, list(range(18)), b) for i in range(4)]
                    if need_ctx:
                        qsets.append((0, CTX, [0, 1], 4))
                    steps = []
                    for qi, (q0, TQ, keys, s) in enumerate(qsets):
                        for comp in (0, 1):
                            for ik, kt in enumerate(keys):
                                steps.append((qi, comp, ik, kt))
                    LA = ATT_LA
                    cur = [0]
                    pend = []
                    live = {}
                    oc_of = {}

                    def defer(delay, fn):
                        pend.append((cur[0] + (delay if ATT_DEFER else 0), len(pend), fn))

                    def tick(flush=False):
                        cur[0] += 1
                        while True:
                            ready = [p_ for p_ in pend if flush or p_[0] <= cur[0]]
                            if not ready:
                                break
                            p_ = min(ready, key=lambda z_: (z_[0], z_[1]))
                            pend.remove(p_)
                            p_[2]()

                    def emit_qk(st):
                        qi, comp, ik, kt = st
                        q0, TQ, keys, s = qsets[qi]
                        bs = nbank((4, 5, 6))
                        kTc = kT0 if comp == 0 else kT1
                        mm(banks[bs][:, 0:TQ], kTc[:, kt * 128:(kt + 1) * 128], qT[:, q0:q0 + TQ], True, True,
                           r=[r_k[kt // 2]] + r_q[q0 // 256:(q0 + TQ) // 256], w=[bres[bs]])
                        ip = PT.get()
                        act(PT.aps[ip][:, 0:TQ], banks[bs][:, 0:TQ], AF.Exp, r=[bres[bs], r_sk], w=[PT.res[ip]],
                            scale=skc[:, kt * 2 + comp: kt * 2 + comp + 1])
                        live[st] = ip

                    def epilogue2(qi):
                        q0, TQ, keys, s = qsets[qi]
                        while pend:
                            p_ = min(pend, key=lambda z_: (z_[0], z_[1]))
                            pend.remove(p_)
                            p_[2]()
                        io0, io1 = oc_of.pop((qi, 0)), oc_of.pop((qi, 1))
                        o32 = FT.aps[io0]
                        stt(o32[:, 0:TQ], FT.aps[io1][:, 0:TQ], L4[:, 1:2], o32[:, 0:TQ], ALU.mult, ALU.add,
                            r=[FT.res[io1], FT.res[io0], r_der], w=[FT.res[io0]])
                        FT.put(io1)
                        st8 = {}

                        def e_sq():
                            st8["osq"] = BT.get()
                            act(BT.aps[st8["osq"]][:, 0:TQ], o32[:, 0:TQ], AF.Square, r=[FT.res[io0]], w=[BT.res[st8["osq"]]])

                        def e_mm():
                            mm(banks[7][:, 0:TQ], ONES128, BT.aps[st8["osq"]][:, 0:TQ], True, True, r=[BT.res[st8["osq"]], r_cm], w=[bres[7]])
                            BT.put(st8["osq"])

                        def e_rs():
                            irs, ilt = FT.get(), FT.get()
                            rsqrt_act(FT.aps[irs][:, 0:TQ], banks[7][:, 0:TQ], [bres[7]], [FT.res[irs]], FT.aps[ilt][:, 0:TQ], [FT.res[ilt]])
                            tt("pool", o32[:, 0:TQ], o32[:, 0:TQ], FT.aps[irs][:, 0:TQ], ALU.mult, r=[FT.res[io0], FT.res[irs]], w=[FT.res[io0]])
                            FT.put(irs)
                            FT.put(ilt)

                        def e_ob():
                            st8["ob"] = BT.get()
                            act(BT.aps[st8["ob"]][:, 0:TQ], o32[:, 0:TQ], AF.Identity, r=[FT.res[io0], r_der], w=[BT.res[st8["ob"]]], scale=L4[:, 2:3])
                            FT.put(io0)

                        def e_wo(n):
                            ob = st8["ob"]
                            mm(banks[7][:, 0:TQ], woap[:, n * 128:(n + 1) * 128], BT.aps[ob][:, 0:TQ], True, True,
                               r=[wores, BT.res[ob]], w=[bres[7]])
                            if SAFE_FORMS:
                                iy = FT.get()
                                act(FT.aps[iy][:, 0:TQ], banks[7][:, 0:TQ], AF.Identity, r=[bres[7], r_mod], w=[FT.res[iy]],
                                    scale=modT[:, l, 16 + n, s:s + 1])
                                tt("pool", xres[:, n, q0:q0 + TQ], xres[:, n, q0:q0 + TQ], FT.aps[iy][:, 0:TQ], ALU.add,
                                   r=[FT.res[iy]] + XR(n, q0, TQ), w=XR(n, q0, TQ))
                                FT.put(iy)
                            else:
                                stt(xres[:, n, q0:q0 + TQ], banks[7][:, 0:TQ], modT[:, l, 16 + n, s:s + 1], xres[:, n, q0:q0 + TQ], ALU.mult, ALU.add,
                                    r=[bres[7], r_mod] + XR(n, q0, TQ), w=XR(n, q0, TQ))
                            if n == NCH - 1:
                                BT.put(ob)

                        defer(3, e_sq)
                        defer(5, e_mm)
                        defer(7, e_rs)
                        defer(9, e_ob)
                        for n in range(NCH):
                            defer(11 + n, lambda n=n: e_wo(n))

                    def emit_pvz(st):
                        qi, comp, ik, kt = st
                        q0, TQ, keys, s = qsets[qi]
                        OB, ZB = (0, 1) if comp == 0 else (2, 3)
                        ip = live.pop(st)
                        first, last_ = ik == 0, ik == len(keys) - 1
                        mm(banks[OB][:, 0:TQ], Vb[:, kt, :], PT.aps[ip][:, 0:TQ], first, last_, r=[r_v[kt // 2], PT.res[ip]], w=[bres[OB]])
                        mm(banks[ZB][:, 0:TQ], ONES1, PT.aps[ip][:, 0:TQ], first, last_, r=[r_cm, PT.res[ip]], w=[bres[ZB]])
                        PT.put(ip)
                        if last_:
                            irz, ioc = FT.get(), FT.get()
                            recip(FT.aps[irz][:, 0:TQ], banks[ZB][:, 0:TQ], r=[bres[ZB]], w=[FT.res[irz]])
                            tt("dve", FT.aps[ioc][:, 0:TQ], banks[OB][:, 0:TQ], FT.aps[irz][:, 0:TQ], ALU.mult,
                               r=[bres[OB], FT.res[irz]], w=[FT.res[ioc]])
                            FT.put(irz)
                            oc_of[(qi, comp)] = ioc
                            if comp == 1:
                                epilogue2(qi)

                    for i, st in enumerate(steps):
                        emit_qk(st)
                        if i >= LA:
                            emit_pvz(steps[i - LA])
                        tick()
                    for st in (steps[-LA:] if LA > 0 else []):
                        emit_pvz(st)
                        tick()
                    tick(flush=True)
                    assert not pend and not live and not oc_of

            for b in range(nb):
                load_x(b)
                for l in range(nlayers):
                    last = l == DEPTH - 1
                    if l % 2 == 0:
                        norm_mod(b, l, "m", with_ctx=not last)
                        conv_mixer(b, l, with_ctx=not last)
                    else:
                        norm_mod(b, l, "m", with_ctx=True)
                        attention(b, l, need_ctx=not last)
                    if stop_mixer and l == nlayers - 1:
                        P.barrier()
                        P.dma("sp", lambda e: e.dma_start(out=dr["dscr"][:, :], in_=scr32[:, :]))
                        continue
                    norm_mod(b, l, "f", with_ctx=not last)
                    ffn(b, l, with_ctx=not last)
                P.barrier()
                store_x(b)
            P.barrier(engines=("sp",))

        if dry_known is None:
            Pd = Planner(nc, es, dry=True)
            WSd = WStream(Pd, wsl, None)
            build(Pd, WSd)
            known = WSd.req
        else:
            known = dry_known
        P = Planner(nc, es)
        WS = WStream(P, wsl, known)
        build(P, WS)
        assert WS.i == len(known), (WS.i, len(known))
        P.emit()
        stats = {"nops": P.nops, "nsem": P.nsem, "per_eng": {e: len(P.ops[e]) for e in ENGS}}
    return nc, stats


_CACHE = {}


def kernel(**inputs):
    inp = {k: np.asarray(v) for k, v in inputs.items()}
    ncore = 8
    B = inp["x"].shape[0]
    nb = B // ncore
    wts = pack_weights(inp)
    cmat, rope = make_consts()
    in_maps = []
    for ci in range(ncore):
        sl = slice(ci * nb, (ci + 1) * nb)
        xT = np.ascontiguousarray(inp["x"][sl].reshape(nb, SEQ, NCH, 128).transpose(0, 3, 2, 1))
        cT = np.ascontiguousarray(inp["ctx"][sl].reshape(nb, CTX, NCH, 128).transpose(0, 3, 2, 1))
        m = {"xT": xT, "ctxT": cT, "cols": pack_cols(inp, inp["c"][sl]), "cmat": cmat, "rope": rope}
        m.update(wts)
        in_maps.append(m)
    nc, _ = build_program(nb=nb)
    res = run_bass_kernel_spmd(nc, in_maps, core_ids=list(range(ncore)))
    outs = []
    for ci in range(ncore):
        o = np.asarray(res.results[ci]["out"])
        outs.append(o.transpose(0, 3, 2, 1).reshape(nb, SEQ, D))
    return np.ascontiguousarray(np.concatenate(outs, axis=0)).astype(np.float32, copy=False)
```
